# Optimizing a Trainium2 kernel written in Bass

```python
import math
import jax
import jax.numpy as jnp
from jax import lax
import numpy as np

D_MODEL = 2048
BATCH = 4
SEQ = 2048
DEPTH = 4
DEC_BATCH = 8
DEC_SEQ = 4096
PAST_LEN = 128

HEAD_DIM = 128
N_HEADS = D_MODEL // HEAD_DIM
N_KV_HEADS = N_HEADS // 4
GQA_GROUP = N_HEADS // N_KV_HEADS
DIFF_HEADS = D_MODEL // (2 * HEAD_DIM)
DIFF_KV_HEADS = DIFF_HEADS // 2
DIFF_GROUP = DIFF_HEADS // DIFF_KV_HEADS
D_FF = 5504
CONV_WIDTH = 3
BLOCK = 128
WINDOW = 128
BAND = BLOCK + 2 * WINDOW
DILATIONS = ((128, 1), (512, 4), (2048, 16))
GRID_W = 64
ROPE_THETA = 10000.0
N_MIXERS = 4
N_MOD = 6
EPS = 1e-6
NEG_INF = -1e30
QKV_WIDTH = (N_HEADS + 2 * N_KV_HEADS) * HEAD_DIM
DIFF_QKV_WIDTH = (DIFF_HEADS + 2 * DIFF_KV_HEADS) * 2 * HEAD_DIM

kernel_name = "hybrid_bidir_encoder_interleaved"


def _n_uses(m):
    return (DEPTH - m + N_MIXERS - 1) // N_MIXERS


def _lambda_init(layer):
    return 0.8 - 0.6 * math.exp(-0.3 * layer)


def rms_norm(x, g):
    xf = x.astype(jnp.float32)
    y = xf * lax.rsqrt(jnp.mean(xf * xf, axis=-1, keepdims=True) + EPS)
    return (y * g.astype(jnp.float32)).astype(x.dtype)


def alibi_slopes(n):
    return 2.0 ** (-8.0 * jnp.arange(1, n + 1, dtype=jnp.float32) / n)


def project_qkv(h, w, n_q, n_kv, d_qk, d_v):
    B, S, _ = h.shape
    qkv = h @ w
    q = qkv[..., :n_q * d_qk].reshape(B, S, n_q, d_qk)
    k = qkv[..., n_q * d_qk:(n_q + n_kv) * d_qk].reshape(B, S, n_kv, d_qk)
    v = qkv[..., (n_q + n_kv) * d_qk:].reshape(B, S, n_kv, d_v)
    return q, k, v


def to_blocks(x):
    B, S = x.shape[:2]
    return jnp.moveaxis(x.reshape(B, S // BLOCK, BLOCK, *x.shape[2:]), 1, 0)


def from_blocks(x):
    nb, B = x.shape[:2]
    return jnp.moveaxis(x, 0, 1).reshape(B, nb * BLOCK, *x.shape[3:])


def axial_rope_tables(S):
    rows = S // GRID_W
    row = jnp.repeat(jnp.arange(rows, dtype=jnp.float32), GRID_W)
    col = jnp.tile(jnp.arange(GRID_W, dtype=jnp.float32), rows)
    n_freq = HEAD_DIM // 4
    inv_freq = ROPE_THETA ** (-jnp.arange(n_freq, dtype=jnp.float32) / n_freq)
    ang = jnp.stack([row, col], axis=-1)[..., None] * inv_freq
    return jnp.cos(ang), jnp.sin(ang)


def apply_axial_rope(x, cos, sin):
    B, S, H, _ = x.shape
    xr = x.reshape(B, S, H, 2, 2, HEAD_DIM // 4)
    x1, x2 = xr[..., 0, :], xr[..., 1, :]
    c, s = cos[None, :, None], sin[None, :, None]
    out = jnp.stack([x1 * c - x2 * s, x1 * s + x2 * c], axis=-2)
    return out.reshape(B, S, H, HEAD_DIM).astype(x.dtype)


def mixer_a(h, wqkv, q_norm, k_norm, wo):
    B, S, _ = h.shape
    q, k, v = project_qkv(h, wqkv, N_HEADS, N_KV_HEADS, HEAD_DIM, HEAD_DIM)
    cos, sin = axial_rope_tables(S)
    q = apply_axial_rope(rms_norm(q, q_norm), cos, sin) * HEAD_DIM ** -0.5
    k = apply_axial_rope(rms_norm(k, k_norm), cos, sin)
    q = q.reshape(B, S, N_KV_HEADS, GQA_GROUP, HEAD_DIM)

    def block(qi):
        s = jnp.einsum('bqkgd,bskd->bkgqs', qi, k).astype(jnp.float32)
        p = jax.nn.softmax(s, axis=-1).astype(v.dtype)
        return jnp.einsum('bkgqs,bskd->bqkgd', p, v)

    o = from_blocks(lax.map(block, to_blocks(q)))
    return o.reshape(B, S, N_HEADS * HEAD_DIM) @ wo


def mixer_b(h, wqkv, sink, wo):
    B, S, _ = h.shape
    q, k, v = project_qkv(h, wqkv, N_HEADS, N_KV_HEADS, HEAD_DIM, HEAD_DIM)
    q = (q * HEAD_DIM ** -0.5).reshape(B, S, N_KV_HEADS, GQA_GROUP, HEAD_DIM)
    pad = ((0, 0), (WINDOW, WINDOW), (0, 0), (0, 0))
    kp, vp = jnp.pad(k, pad), jnp.pad(v, pad)
    slopes = alibi_slopes(N_HEADS).reshape(N_KV_HEADS, GQA_GROUP)
    sink_f = sink.astype(jnp.float32).reshape(N_KV_HEADS, GQA_GROUP)
    rel = jnp.arange(BAND)[None, :] - WINDOW - jnp.arange(BLOCK)[:, None]
    alibi = -slopes[:, :, None, None] * jnp.abs(rel).astype(jnp.float32)
    in_window = jnp.abs(rel) <= WINDOW
    starts = jnp.arange(S // BLOCK) * BLOCK

    def block(args):
        qi, start = args
        kb = lax.dynamic_slice_in_dim(kp, start, BAND, axis=1)
        vb = lax.dynamic_slice_in_dim(vp, start, BAND, axis=1)
        key_pos = start - WINDOW + jnp.arange(BAND)
        valid = in_window & ((key_pos >= 0) & (key_pos < S))[None, :]
        s = jnp.einsum('bqkgd,bskd->bkgqs', qi, kb).astype(jnp.float32) + alibi
        s = jnp.where(valid, s, NEG_INF)
        sink_col = jnp.broadcast_to(sink_f[:, :, None, None], s.shape[:-1] + (1,))
        p = jax.nn.softmax(jnp.concatenate([s, sink_col], axis=-1), axis=-1)[..., :-1]
        return jnp.einsum('bkgqs,bskd->bqkgd', p.astype(v.dtype), vb)

    o = from_blocks(lax.map(block, (to_blocks(q), starts)))
    return o.reshape(B, S, N_HEADS * HEAD_DIM) @ wo


def mixer_c(h, wqkv, lam_params, subln, wo, layer):
    B, S, _ = h.shape
    lam_init = _lambda_init(layer)
    q, k, v = project_qkv(h, wqkv, DIFF_HEADS, DIFF_KV_HEADS, 2 * HEAD_DIM, 2 * HEAD_DIM)
    q = (q * HEAD_DIM ** -0.5).reshape(B, S, DIFF_KV_HEADS, DIFF_GROUP, 2, HEAD_DIM)
    k = k.reshape(B, S, DIFF_KV_HEADS, 2, HEAD_DIM)
    lp = lam_params.astype(jnp.float32)
    lam = jnp.exp(jnp.sum(lp[0] * lp[1])) - jnp.exp(jnp.sum(lp[2] * lp[3])) + lam_init
    slopes = alibi_slopes(DIFF_HEADS).reshape(DIFF_KV_HEADS, DIFF_GROUP)
    key_pos = jnp.arange(S)
    starts = jnp.arange(S // BLOCK) * BLOCK

    def block(args):
        qi, start = args
        s = jnp.einsum('bqkgmd,bskmd->bkgmqs', qi, k).astype(jnp.float32)
        dist = jnp.abs(start + jnp.arange(BLOCK)[:, None] - key_pos[None, :]).astype(jnp.float32)
        s = s - slopes[:, :, None, None, None] * dist
        p = jax.nn.softmax(s, axis=-1)
        a = (p[:, :, :, 0] - lam * p[:, :, :, 1]).astype(v.dtype)
        return jnp.einsum('bkgqs,bskd->bqkgd', a, v)

    o = from_blocks(lax.map(block, (to_blocks(q), starts)))
    o = rms_norm(o, subln) * (1.0 - lam_init)
    return o.reshape(B, S, DIFF_HEADS * 2 * HEAD_DIM) @ wo


def mixer_d(h, wqkv, wo):
    B, S, _ = h.shape
    q, k, v = project_qkv(h, wqkv, N_HEADS, N_KV_HEADS, HEAD_DIM, HEAD_DIM)
    q = (q * HEAD_DIM ** -0.5).reshape(B, S, N_KV_HEADS, GQA_GROUP, HEAD_DIM)
    slopes = alibi_slopes(N_HEADS).reshape(N_KV_HEADS, GQA_GROUP)
    starts = jnp.arange(S // BLOCK) * BLOCK

    def block(args):
        qi, start = args
        t = start + jnp.arange(BLOCK)
        outs, lses = [], []
        for window, dil in DILATIONS:
            n_side = (window // 2) // dil
            off = dil * jnp.arange(-n_side, n_side + 1)
            idx = t[:, None] + off[None, :]
            valid = (idx >= 0) & (idx < S)
            idx = jnp.clip(idx, 0, S - 1)
            kg = jnp.take(k, idx, axis=1)
            vg = jnp.take(v, idx, axis=1)
            s = jnp.einsum('bqkgd,bqjkd->bkgqj', qi, kg).astype(jnp.float32)
            s = s - slopes[:, :, None, None] * jnp.abs(off).astype(jnp.float32)
            s = jnp.where(valid, s, NEG_INF)
            lse = jax.nn.logsumexp(s, axis=-1)
            p = jnp.exp(s - lse[..., None]).astype(v.dtype)
            outs.append(jnp.einsum('bkgqj,bqjkd->bqkgd', p, vg))
            lses.append(jnp.moveaxis(lse, -1, 1))
        wgt = jax.nn.softmax(jnp.stack(lses), axis=0).astype(v.dtype)
        return jnp.einsum('nbqkg,nbqkgd->bqkgd', wgt, jnp.stack(outs))

    o = from_blocks(lax.map(block, (to_blocks(q), starts)))
    return o.reshape(B, S, N_HEADS * HEAD_DIM) @ wo


def conv_glu(h, w_up, conv_w, conv_b, w_down):
    S = h.shape[1]
    u = h @ w_up
    a, b = u[..., :D_FF], u[..., D_FF:]
    half = CONV_WIDTH // 2
    ap = jnp.pad(a, ((0, 0), (half, half), (0, 0)))
    a = conv_b + sum(ap[:, j:j + S] * conv_w[j] for j in range(CONV_WIDTH))
    return (jax.nn.gelu(a, approximate=False) * b) @ w_down


def trunk(x, c, p):
    for i in range(DEPTH):
        mod = jax.nn.silu(c) @ p['w_ada'][i] + p['b_ada'][i]
        sh1, sc1, g1, sh2, sc2, g2 = jnp.split(mod[:, None, :], N_MOD, axis=-1)
        h = rms_norm(x, p['norm_attn'][i]) * (1.0 + sc1) + sh1
        m, j = i % N_MIXERS, i // N_MIXERS
        if m == 0:
            y = mixer_a(h, p['a_wqkv'][j], p['a_q_norm'][j], p['a_k_norm'][j], p['a_wo'][j])
        elif m == 1:
            y = mixer_b(h, p['b_wqkv'][j], p['b_sink'][j], p['b_wo'][j])
        elif m == 2:
            y = mixer_c(h, p['c_wqkv'][j], p['c_lambda'][j], p['c_subln'][j], p['c_wo'][j], i)
        else:
            y = mixer_d(h, p['d_wqkv'][j], p['d_wo'][j])
        x = x + g1 * y
        h = rms_norm(x, p['norm_ffn'][i]) * (1.0 + sc2) + sh2
        x = x + g2 * conv_glu(h, p['ffn_w_up'][i], p['ffn_conv_w'][i], p['ffn_conv_b'][i], p['ffn_w_down'][i])
    return rms_norm(x, p['norm_final'])


def _normal(key, shape, scale):
    return scale * jax.random.normal(key, shape, dtype=jnp.float32)


def setup_inputs(seed: int = 0) -> dict:
    key = jax.random.key(seed)
    ks = jax.random.split(key, 26)
    la, lb, lc, ld = (_n_uses(m) for m in range(N_MIXERS))
    D = D_MODEL
    inv = D ** -0.5
    attn_out = (N_HEADS * HEAD_DIM) ** -0.5
    return {
        'x_prompt': _normal(ks[0], (BATCH, SEQ, D), 1.0),
        'x_sample': _normal(ks[1], (DEC_BATCH, DEC_SEQ, D), 1.0),
        'c_prompt': _normal(ks[2], (BATCH, D), 1.0),
        'c_sample': _normal(ks[3], (DEC_BATCH, D), 1.0),
        'norm_attn': 1.0 + _normal(ks[4], (DEPTH, D), 0.05),
        'norm_ffn': 1.0 + _normal(ks[5], (DEPTH, D), 0.05),
        'w_ada': _normal(ks[6], (DEPTH, D, N_MOD * D), 0.5 * inv),
        'b_ada': _normal(ks[7], (DEPTH, N_MOD * D), 0.02),
        'a_wqkv': _normal(ks[8], (la, D, QKV_WIDTH), inv),
        'a_q_norm': 1.0 + _normal(ks[9], (la, HEAD_DIM), 0.05),
        'a_k_norm': 1.0 + _normal(ks[10], (la, HEAD_DIM), 0.05),
        'a_wo': _normal(ks[11], (la, N_HEADS * HEAD_DIM, D), attn_out),
        'b_wqkv': _normal(ks[12], (lb, D, QKV_WIDTH), inv),
        'b_sink': _normal(ks[13], (lb, N_HEADS), 0.5),
        'b_wo': _normal(ks[14], (lb, N_HEADS * HEAD_DIM, D), attn_out),
        'c_wqkv': _normal(ks[15], (lc, D, DIFF_QKV_WIDTH), inv),
        'c_lambda': _normal(ks[16], (lc, 4, HEAD_DIM), 0.1),
        'c_subln': 1.0 + _normal(ks[17], (lc, 2 * HEAD_DIM), 0.05),
        'c_wo': _normal(ks[18], (lc, DIFF_HEADS * 2 * HEAD_DIM, D), attn_out),
        'd_wqkv': _normal(ks[19], (ld, D, QKV_WIDTH), inv),
        'd_wo': _normal(ks[20], (ld, N_HEADS * HEAD_DIM, D), attn_out),
        'ffn_w_up': _normal(ks[21], (DEPTH, D, 2 * D_FF), inv),
        'ffn_conv_w': _normal(ks[22], (DEPTH, CONV_WIDTH, D_FF), CONV_WIDTH ** -0.5),
        'ffn_conv_b': _normal(ks[23], (DEPTH, D_FF), 0.02),
        'ffn_w_down': _normal(ks[24], (DEPTH, D_FF, D), D_FF ** -0.5),
        'norm_final': 1.0 + _normal(ks[25], (D,), 0.05),
    }


def reference(x_prompt, x_sample, c_prompt, c_sample, norm_attn, norm_ffn, w_ada, b_ada,
              a_wqkv, a_q_norm, a_k_norm, a_wo, b_wqkv, b_sink, b_wo,
              c_wqkv, c_lambda, c_subln, c_wo, d_wqkv, d_wo,
              ffn_w_up, ffn_conv_w, ffn_conv_b, ffn_w_down, norm_final):
    p = dict(norm_attn=norm_attn, norm_ffn=norm_ffn, w_ada=w_ada, b_ada=b_ada,
             a_wqkv=a_wqkv, a_q_norm=a_q_norm, a_k_norm=a_k_norm, a_wo=a_wo,
             b_wqkv=b_wqkv, b_sink=b_sink, b_wo=b_wo,
             c_wqkv=c_wqkv, c_lambda=c_lambda, c_subln=c_subln, c_wo=c_wo,
             d_wqkv=d_wqkv, d_wo=d_wo,
             ffn_w_up=ffn_w_up, ffn_conv_w=ffn_conv_w, ffn_conv_b=ffn_conv_b,
             ffn_w_down=ffn_w_down, norm_final=norm_final)
    y_prompt = trunk(x_prompt, c_prompt, p)
    y_sample = trunk(x_sample, c_sample, p)
    return (y_prompt, y_sample)
```

```python
import math
import os
from contextlib import ExitStack

import numpy as np
import concourse.bass as bass
import concourse.mybir as mybir
from concourse.bass_utils import run_bass_kernel_spmd

F32 = mybir.dt.float32
BF16 = mybir.dt.bfloat16
AF = mybir.ActivationFunctionType
ALU = mybir.AluOpType

D = 2048
KC = 16
DFF = 5504
FC = 43
HD = 128
EPS = 1e-6
DEPTH = 4
SCALE = HD ** -0.5
N_CORES = 8


STQ = os.environ.get("K_STQ", "pool")
NODEDUP = os.environ.get("K_NODEDUP", "0") == "1"


class Sched:
    ENGS = ("pe", "act", "dve", "pool", "sp")

    def __init__(self, nc, es, n_dma_sems=56):
        self.nc = nc
        self.ops = {e: [] for e in self.ENGS}
        self.esem = {e: es.enter_context(nc.semaphore("s_" + e)) for e in ("pe", "act", "dve", "pool")}
        self.ecount = {e: 0 for e in self.ENGS}
        self.dsem = [es.enter_context(nc.semaphore("d%d" % i)) for i in range(n_dma_sems)]
        self.dcount = [0] * n_dma_sems
        self.dnext = 0
        self.dnext_pool = 0
        self.waited = {e: {} for e in self.ENGS}
        self.lastw = {}
        self.readers = {}

    def _deps(self, reads, writes):
        deps = {}

        def add(k, v):
            if deps.get(k, 0) < v:
                deps[k] = v

        for b in reads:
            t = self.lastw.get(b)
            if t:
                add(*t)
        for b in writes:
            t = self.lastw.get(b)
            if t:
                add(*t)
            for k, v in self.readers.get(b, {}).items():
                add(k, v)
        return deps

    def _reduce(self, eng, deps):
        out = []
        w = self.waited[eng]
        for k, v in deps.items():
            if k == ("e", "pe") and eng == "pe":
                continue
            if w.get(k, 0) < v or (NODEDUP and k[0] == "d" and eng in ("dve", "act")):
                w[k] = max(v, w.get(k, 0))
                out.append((k, v))
        return out

    def _commit(self, tok, reads, writes):
        for b in writes:
            self.lastw[b] = tok
            self.readers[b] = {}
        for b in reads:
            r = self.readers.setdefault(b, {})
            if r.get(tok[0], 0) < tok[1]:
                r[tok[0]] = tok[1]

    def op(self, eng, fn, reads=(), writes=(), signal=True):
        waits = self._reduce(eng, self._deps(reads, writes))
        if signal:
            self.ecount[eng] += 1
            tok = (("e", eng), self.ecount[eng])
        else:
            tok = (("e", eng), self.ecount[eng] + 1)
        self.ops[eng].append((waits, fn, True if signal else None))
        self._commit(tok, reads, writes)
        return tok

    def dma(self, q, out, in_, reads=(), writes=()):
        NP = 24
        if q == "pool":
            i = self.dnext_pool
            self.dnext_pool = (i + 1) % NP
        else:
            i = NP + self.dnext
            self.dnext = (self.dnext + 1) % (len(self.dsem) - NP)
        deps = self._deps(reads, writes)
        if self.dcount[i]:
            k = ("d", i)
            if deps.get(k, 0) < self.dcount[i]:
                deps[k] = self.dcount[i]
        waits = self._reduce(q, deps)
        self.dcount[i] += 16
        tok = (("d", i), self.dcount[i])
        self.ops[q].append((waits, (lambda e, o=out, s=in_: e.dma_start(out=o, in_=s)), ("d", i)))
        self._commit(tok, reads, writes)
        return tok

    def _sem(self, k):
        return self.esem[k[1]] if k[0] == "e" else self.dsem[k[1]]

    def finish(self):
        deps = {}
        for i, c in enumerate(self.dcount):
            if c:
                deps[("d", i)] = c
        for e in ("pe", "act", "dve", "pool"):
            if self.ecount[e]:
                deps[("e", e)] = self.ecount[e]
        waits = self._reduce("sp", deps)
        self.ops["sp"].append((waits, None, None))

    def _emit(self, name, e):
        for waits, fn, sig in self.ops[name]:
            for k, v in waits:
                e.wait_ge(self._sem(k), v)
            if fn is None:
                continue
            ins = fn(e)
            if sig is True:
                ins.then_inc(self.esem[name], 1)
            elif sig is not None:
                ins.then_inc(self.dsem[sig[1]], 16)

    def emit(self):
        with self.nc.Block() as block:
            @block.tensor
            def _(e):
                self._emit("pe", e)

            @block.scalar
            def _(e):
                self._emit("act", e)

            @block.vector
            def _(e):
                self._emit("dve", e)

            @block.gpsimd
            def _(e):
                self._emit("pool", e)

            @block.sync
            def _(e):
                self._emit("sp", e)


def _alibi_slopes(n):
    return 2.0 ** (-8.0 * np.arange(1, n + 1, dtype=np.float64) / n)


TB_CEN, TB_L = 512, 1152
TD_CEN, TD_L = 1408, 2944
TC_CEN, TC_L = 3968, 8064


def _toeplitz(fvals_fn, cen, L):
    i = np.arange(128)[:, None]
    c = np.arange(L)[None, :]
    d = i - c + cen
    return fvals_fn(d)


def _const_tables():
    sl16 = _alibi_slopes(16)
    sl8 = _alibi_slopes(8)
    tb = np.zeros((16, 128, TB_L), np.float32)
    td = np.zeros((16, 128, TD_L), np.float32)
    tc = np.zeros((8, 128, TC_L), np.float32)

    def mult_d(d):
        a = np.abs(d)
        m = (a <= 64).astype(np.float64)
        m += ((a <= 256) & (a % 4 == 0)).astype(np.float64)
        m += ((a <= 1024) & (a % 16 == 0)).astype(np.float64)
        return m

    for h in range(16):
        tb[h] = _toeplitz(lambda d: (np.abs(d) <= 128) * np.exp(-sl16[h] * np.abs(d)), TB_CEN, TB_L)
        td[h] = _toeplitz(lambda d: mult_d(d) * np.exp(-sl16[h] * np.abs(d)), TD_CEN, TD_L)
    for h in range(8):
        tc[h] = _toeplitz(lambda d: np.exp(-sl8[h] * np.abs(d)), TC_CEN, TC_L)
    t = np.arange(4096, dtype=np.float64)
    row, col = np.floor(t / 64), t % 64
    inv = 10000.0 ** (-np.arange(32, dtype=np.float64) / 32)
    rc = np.zeros((128, 4096), np.float32)
    rs = np.zeros((128, 4096), np.float32)
    pm = np.zeros((128, 128), np.float32)
    for dd in range(128):
        a, m, f = dd // 64, (dd // 32) % 2, dd % 32
        ang = (row if a == 0 else col) * inv[f]
        rc[dd] = np.cos(ang)
        rs[dd] = np.sin(ang) * (-1.0 if m == 0 else 1.0)
        src = a * 64 + (1 - m) * 32 + f
        pm[src, dd] = 1.0
    ident = np.eye(128, dtype=np.float32)
    return dict(tb=tb, td=td, tc=tc, rope_c=rc, rope_s=rs, pm=pm, ident=ident)


def _lambda_init(layer):
    return 0.8 - 0.6 * math.exp(-0.3 * layer)


def build_program(S_list, depth=DEPTH):
    nc = bass.Bass("TRN2", target_bir_lowering=False)
    es = ExitStack()
    NSEQ = len(S_list)

    def din(name, shape, dt=F32):
        return nc.dram_tensor(name, list(shape), dt, kind="ExternalInput").ap()

    def dscr(name, shape, dt):
        return nc.dram_tensor(name, list(shape), dt, kind="Internal").ap()

    x_in = [din("x%d" % s, [S, D]) for s, S in enumerate(S_list)]
    c_in = [din("c%d" % s, [KC, 128]) for s in range(NSEQ)]
    y_out = [nc.dram_tensor("y%d" % s, [S, D], F32, kind="ExternalOutput").ap() for s, S in enumerate(S_list)]
    norm_attn = din("norm_attn", [DEPTH, KC, 128])
    norm_ffn = din("norm_ffn", [DEPTH, KC, 128])
    w_ada = din("w_ada", [DEPTH, D, 6 * D])
    b_ada = din("b_ada", [DEPTH, 96, 128])
    wqkv = [din("a_wqkv", [D, 3072]), din("b_wqkv", [D, 3072]), din("c_wqkv", [D, 4096]), din("d_wqkv", [D, 3072])]
    wo = [din("a_wo", [D, D]), din("b_wo", [D, D]), din("c_wo", [D, D]), din("d_wo", [D, D])]
    a_q_norm = din("a_q_norm", [1, 128])
    a_k_norm = din("a_k_norm", [1, 128])
    b_sink = din("b_sink", [1, 16])
    c_lambda = din("c_lambda", [4, 128])
    c_subln = din("c_subln", [2, 128])
    w_up = din("ffn_w_up", [DEPTH, D, 2 * DFF])
    conv_w = din("ffn_conv_w", [DEPTH, 3, FC, 128])
    conv_b = din("ffn_conv_b", [DEPTH, FC, 128])
    w_down = din("ffn_w_down", [DEPTH, DFF, D])
    norm_final = din("norm_final", [KC, 128])
    tb_d = din("tb", [16, 128, TB_L])
    td_d = din("td", [16, 128, TD_L])
    tc_d = din("tc", [8, 128, TC_L])
    rope_c_d = din("rope_c", [128, 4096])
    rope_s_d = din("rope_s", [128, 4096])
    pm_d = din("pm", [128, 128])
    ident_d = din("ident", [128, 128])

    XT = [dscr("XT%d" % s, [KC, 128, S], F32) for s, S in enumerate(S_list)]
    HT = [dscr("HT%d" % s, [KC, 128, S], BF16) for s, S in enumerate(S_list)]
    QT = [dscr("QT%d" % s, [16, 128, S], BF16) for s, S in enumerate(S_list)]
    KT = [dscr("KT%d" % s, [8, 128, S], BF16) for s, S in enumerate(S_list)]
    VV = [dscr("VV%d" % s, [S, 1024], BF16) for s, S in enumerate(S_list)]
    OT = [dscr("OT%d" % s, [16, 128, S], BF16) for s, S in enumerate(S_list)]
    GT = [dscr("GT%d" % s, [FC, 128, S], BF16) for s, S in enumerate(S_list)]

    def sb(name, shape, dt):
        return es.enter_context(nc.sbuf_tensor(name, list(shape), dt))

    WB = [sb("WB0", [128, 24576], BF16), sb("WB1", [128, 24576], BF16)]
    RA = [sb("RA0", [128, 8192], BF16), sb("RA1", [128, 8192], BF16)]
    RB = sb("RB", [128, 11264], F32)
    NF32 = 6
    NB16 = 6
    RSTD = [sb("RSTD%d" % i, [128, 256], F32) for i in range(2)]
    QTILE = sb("QTILE", [128, 1024], BF16)
    SF = [sb("SF%d" % i, [128, 512], F32) for i in range(NF32)]
    SH = [sb("SH%d" % i, [128, 512], BF16) for i in range(NB16)]
    IDENT = sb("IDENT", [128, 128], F32)
    ONESB = sb("ONESB", [128, 128], BF16)
    ONESF = sb("ONESF", [128, 128], F32)
    PMB = sb("PMB", [128, 128], BF16)
    PMF = sb("PMF", [128, 128], F32)
    MODP = sb("MODP", [128, NSEQ * DEPTH * 6 * KC], F32)
    MODRAW = sb("MODRAW", [128, 96 * NSEQ], F32)
    BADA = sb("BADA", [128, 96], F32)
    NAT = sb("NAT", [128, 2 * KC], F32)
    SCT = sb("SCT", [128, KC * NSEQ], F32)
    CT = sb("CT", [128, KC * NSEQ], F32)
    CONVW = sb("CONVW", [128, DEPTH * 3 * FC], F32)
    CONVB = sb("CONVB", [128, DEPTH * FC], F32)
    NFIN = sb("NFIN", [128, KC], F32)
    SMALL = sb("SMALL", [128, 64], F32)
    ROWS = sb("ROWS", [128, 128], F32)
    PS = [es.enter_context(nc.psum_tensor("PS%d" % i, [128, 512], F32)) for i in range(8)]

    sch = Sched(nc, es)
    ctr = {"sf": 0, "sh": 0, "ps": 0, "ev": 0}

    def sf():
        i = ctr["sf"]
        ctr["sf"] = (i + 1) % NF32
        return SF[i], ("SF", i)

    def sh():
        i = ctr["sh"]
        ctr["sh"] = (i + 1) % NB16
        return SH[i], ("SH", i)

    def psb(lo=0, hi=8):
        i = ctr["ps"]
        if i < lo or i >= hi:
            i = lo
        ctr["ps"] = i + 1 if i + 1 < hi else lo
        return PS[i], ("PS", i)

    slot_ctr = {"RA": 0, "RB": 0}

    def next_slot(name):
        u = slot_ctr[name]
        slot_ctr[name] ^= 1
        return u

    def evac_eng():
        ctr["ev"] ^= 1
        return "act" if ctr["ev"] else "dve"

    def copy_op(eng, out, in_, reads, writes):
        if eng == "act":
            return sch.op("act", lambda e: e.activation(out=out, in_=in_, func=AF.Copy), reads, writes)
        return sch.op(eng, lambda e: e.tensor_copy(out, in_), reads, writes)

    def mm(out, lhsT, rhs, start, stop, reads, writes, force=False):
        return sch.op("pe", lambda e: e.matmul(out, lhsT, rhs, start=start, stop=stop), reads, writes, signal=(stop or force))

    sch.dma("sp", IDENT[:], ident_d, writes=("IDENT",))
    sch.dma("sp", PMF[:], pm_d, writes=("PMF",))
    sch.op("dve", lambda e: e.tensor_copy(PMB[:], PMF[:]), reads=("PMF",), writes=("PMB",))
    sch.op("dve", lambda e: e.memset(ONESB[:], 1.0), writes=("ONESB",))
    sch.op("dve", lambda e: e.memset(ONESF[:], 1.0), writes=("ONESF",))

    def load_T(src_rows, nrows, dst, dst_key):
        sch.dma("sp", ROWS[0:nrows, :], src_rows, writes=("ROWS",))
        ps, pk = psb()
        sch.op("pe", lambda e: e.transpose(ps[:, 0:nrows], ROWS[0:nrows, :], IDENT[0:nrows, 0:nrows]),
               reads=("ROWS", "IDENT"), writes=(pk,))
        sch.op("dve", lambda e: e.tensor_copy(dst, ps[:, 0:nrows]), reads=(pk,), writes=(dst_key,))

    load_T(norm_final, KC, NFIN[:, :], "NFIN")
    load_T(a_q_norm, 1, SMALL[:, 0:1], "SMALL_QN")
    load_T(a_k_norm, 1, SMALL[:, 1:2], "SMALL_KN")
    load_T(c_subln, 2, SMALL[:, 2:4], "SMALL_SUBW")
    sch.op("dve", lambda e: e.tensor_scalar(SMALL[:, 2:4], SMALL[:, 2:4], 1.0 - _lambda_init(2), None, op0=ALU.mult),
           reads=("SMALL_SUBW",), writes=("SMALL_SUBW",))
    load_T(c_lambda, 4, SMALL[:, 24:28], "SMALL_LT")
    sch.op("dve", lambda e: e.tensor_tensor(SMALL[:, 28:29], SMALL[:, 24:25], SMALL[:, 25:26], ALU.mult),
           reads=("SMALL_LT",), writes=("SMALL_LP0",))
    sch.op("dve", lambda e: e.tensor_tensor(SMALL[:, 29:30], SMALL[:, 26:27], SMALL[:, 27:28], ALU.mult),
           reads=("SMALL_LT",), writes=("SMALL_LP1",))
    ps, pk = psb()
    mm(ps[:, 0:2], ONESF[:], SMALL[:, 28:30], True, True, reads=("ONESF", "SMALL_LP0", "SMALL_LP1"), writes=(pk,))
    sch.op("act", lambda e, ps=ps: e.activation(out=SMALL[:, 30:32], in_=ps[:, 0:2], func=AF.Exp),
           reads=(pk,), writes=("SMALL_LE",))
    sch.op("dve", lambda e: e.tensor_tensor(SMALL[:, 4:5], SMALL[:, 31:32], SMALL[:, 30:31], ALU.subtract),
           reads=("SMALL_LE",), writes=("SMALL_NL",))
    sch.op("dve", lambda e: e.tensor_scalar(SMALL[:, 4:5], SMALL[:, 4:5], -_lambda_init(2), None, op0=ALU.add),
           reads=("SMALL_NL",), writes=("SMALL_NL",))
    sch.dma("sp", SMALL[:, 8:24], b_sink.partition_broadcast(128), writes=("SMALL_ES",))
    sch.op("act", lambda e: e.activation(out=SMALL[:, 8:24], in_=SMALL[:, 8:24], func=AF.Exp),
           reads=("SMALL_ES",), writes=("SMALL_ES",))
    for l in range(depth):
        for j in range(3):
            load_T(conv_w[l, j], FC, CONVW[:, (l * 3 + j) * FC:(l * 3 + j + 1) * FC], ("CONVW", l, j))
        load_T(conv_b[l], FC, CONVB[:, l * FC:(l + 1) * FC], ("CONVB", l))

    for s in range(NSEQ):
        load_T(c_in[s], KC, CT[:, s * KC:(s + 1) * KC], ("CT", s))
    sch.op("act", lambda e: e.activation(out=SCT[:, :], in_=CT[:, :], func=AF.Sigmoid),
           reads=[("CT", s) for s in range(NSEQ)], writes=("SCT",))
    sch.op("dve", lambda e: e.tensor_tensor(SCT[:, :], SCT[:, :], CT[:, :], ALU.mult),
           reads=[("CT", s) for s in range(NSEQ)] + ["SCT"], writes=("SCT",))
    SCTv = SCT[:, :].rearrange("p (s k) -> p k s", s=NSEQ)
    WBF = [WB[0][:, :].bitcast(F32), WB[1][:, :].bitcast(F32)]
    ADA_NC = 768

    def modp(s, l, which):
        o = ((s * DEPTH + l) * 6 + which) * KC
        return MODP[:, o:o + KC]

    slab = 0
    for l in range(depth):
        load_T(b_ada[l], 96, BADA[:, :], "BADA")
        load_T(norm_attn[l], KC, NAT[:, 0:KC], "NAT0")
        load_T(norm_ffn[l], KC, NAT[:, KC:2 * KC], "NAT1")
        ps, pk = psb()
        for sl in range(6 * D // ADA_NC):
            h = slab % 2
            slab += 1
            wv = WBF[h][:, 0:KC * ADA_NC].rearrange("p (k n) -> p k n", k=KC)
            for g in range(4):
                sch.dma("sp", wv[:, g * 4:(g + 1) * 4, :],
                        w_ada[l, g * 512:(g + 1) * 512, sl * ADA_NC:(sl + 1) * ADA_NC].rearrange("(k p) n -> p k n", p=128),
                        writes=(("WB", h, g),))
            for j in range(ADA_NC // 128):
                oc = sl * (ADA_NC // 128) + j
                for kc in range(KC):
                    mm(ps[:, oc * NSEQ:(oc + 1) * NSEQ], wv[:, kc, j * 128:(j + 1) * 128], SCTv[:, kc, :],
                       kc == 0, kc == KC - 1, reads=(("WB", h, kc // 4), "SCT"), writes=(pk,))
        for s in range(NSEQ):
            sch.op("dve", lambda e, s=s, ps=ps: e.tensor_tensor(
                MODRAW[:, s * 96:(s + 1) * 96], ps[:, 0:96 * NSEQ].rearrange("p (o s) -> p s o", s=NSEQ)[:, s, :],
                BADA[:, :], ALU.add), reads=(pk, "BADA"), writes=(("MODRAW", s),))
            R = lambda w, s=s: MODRAW[:, s * 96 + w * KC: s * 96 + (w + 1) * KC]
            for (dst, a_src, nat) in ((0, 1, 0), (3, 4, 1)):
                sch.op("dve", lambda e, dst=dst, a_src=a_src, nat=nat, s=s, l=l, R=R: e.scalar_tensor_tensor(
                    modp(s, l, dst), R(a_src), 1.0, NAT[:, nat * KC:(nat + 1) * KC], ALU.add, ALU.mult),
                    reads=(("MODRAW", s), "NAT0", "NAT1"), writes=(("MODP", s, l, dst),))
            for (dst, src) in ((1, 0), (2, 2), (4, 3), (5, 5)):
                sch.op("dve", lambda e, dst=dst, src=src, s=s, l=l, R=R: e.tensor_copy(modp(s, l, dst), R(src)),
                       reads=(("MODRAW", s),), writes=(("MODP", s, l, dst),))

    sch.op("dve", lambda e: e.memset(SMALL[:, 40:41], 0.0),
           reads=[("MODP", s_, l_, w_) for s_ in range(NSEQ) for l_ in range(depth) for w_ in range(6)], writes=("MODPALL",))

    def wb_view(h, kcn, ncols):
        return WB[h][:, 0:kcn * ncols].rearrange("p (k n) -> p k n", k=kcn)

    def load_w(h, src2d, k0, kcn, col_ranges):
        ncols = sum(c1 - c0 for c0, c1 in col_ranges)
        wv = wb_view(h, kcn, ncols)
        step = 4
        for g0 in range(0, kcn, step):
            g1 = min(kcn, g0 + step)
            o = 0
            for c0, c1 in col_ranges:
                sch.dma("pool", wv[:, g0:g1, o:o + (c1 - c0)],
                        src2d[(k0 + g0) * 128:(k0 + g1) * 128, c0:c1].rearrange("(k p) n -> p k n", p=128),
                        writes=(("WB", h, g0 // step), ("WB", h, "att", 0), ("WB", h, "att", 1)))
                o += c1 - c0
        return wv

    def wkeys(h, kc, col_ranges):
        return [("WB", h, kc // 4)]

    wb_next = [0]

    def next_wb():
        h = wb_next[0]
        wb_next[0] ^= 1
        return h

    def rstd_from_ps(ps, pk, T, n_feat, dst=None):
        r, rk = dst if dst is not None else sf()
        sch.op("act", lambda e: e.activation(out=r[:, 0:T], in_=ps[:, 0:T], func=AF.Sqrt, bias=EPS, scale=1.0 / n_feat),
               reads=(pk,), writes=(rk,))
        sch.op("dve", lambda e: e.reciprocal(r[:, 0:T], r[:, 0:T]), reads=(rk,), writes=(rk,))
        return r, rk

    def phase_transpose_in(s):
        S = S_list[s]
        T = 256
        RBv = RB[:, :].rearrange("p (u f) -> p u f", u=2)[:, :, 0:4096].rearrange("p u (b f) -> p u b f", b=2)
        nt = S // T
        us = [next_slot("RB") for _ in range(nt)]

        def load(ti):
            for b in range(2):
                sch.dma("sp", RBv[:, us[ti], b, :], x_in[s][ti * T + b * 128:ti * T + (b + 1) * 128, :], writes=(("RB", us[ti]),))

        load(0)
        for ti in range(nt):
            u = us[ti]
            t0 = ti * T
            if ti + 1 < nt:
                load(ti + 1)
            for kc in range(KC):
                ps, pk = psb()
                for b in range(2):
                    sch.op("pe", lambda e, ps=ps, b=b, u=u, kc=kc: e.transpose(
                        ps[:, b * 128:(b + 1) * 128], RBv[:, u, b, kc * 128:(kc + 1) * 128], IDENT[:, :]),
                        reads=(("RB", u), "IDENT"), writes=(pk,), signal=(b == 1))
                st, sk = sf()
                copy_op(evac_eng(), st[:, 0:T], ps[:, 0:T], (pk,), (sk,))
                sch.dma(STQ, XT[s][kc, :, t0:t0 + T], st[:, 0:T], reads=(sk,), writes=(("XT", s, kc),))

    def phase_norm(s, A_ap, B_ap, final=False):
        S = S_list[s]
        T = 256
        RBv = RB[:, :].rearrange("p (u f) -> p u f", u=2)[:, :, 0:4096].rearrange("p u (k t) -> p u k t", k=KC)
        nt = S // T
        us = [next_slot("RB") for _ in range(nt)]
        uas = [next_slot("RA") for _ in range(nt)]

        def load(ti):
            sch.dma("sp", RBv[:, us[ti], :, :], XT[s][:, :, ti * T:(ti + 1) * T].rearrange("k p t -> p k t"),
                    reads=[("XT", s, kc) for kc in range(KC)], writes=(("RB", us[ti]),))

        load(0)
        for ti in range(nt):
            u = us[ti]
            ua = uas[ti]
            t0 = ti * T
            if ti + 1 < nt:
                load(ti + 1)
            ps, pk = psb()
            for kc in range(KC):
                q, qk = sh()
                sch.op("act", lambda e, q=q, u=u, kc=kc: e.activation(out=q[:, 0:T], in_=RBv[:, u, kc, :], func=AF.Square),
                       reads=(("RB", u),), writes=(qk,))
                mm(ps[:, 0:T], ONESB[:, :], q[:, 0:T], kc == 0, kc == KC - 1, reads=(qk, "ONESB"), writes=(pk,), force=True)
            r, rk = rstd_from_ps(ps, pk, T, D, dst=(RSTD[u], ("RSTD", u)))
            if not final:
                hout = RA[ua][:, 0:KC * T].rearrange("p (k t) -> p k t", k=KC)
                for kc in range(KC):
                    tm, tk = sf()
                    sch.op("dve", lambda e, tm=tm, u=u, kc=kc, r=r: e.tensor_tensor(tm[:, 0:T], RBv[:, u, kc, :], r[:, 0:T], ALU.mult),
                           reads=(("RB", u), rk), writes=(tk,))
                    sch.op("act", lambda e, tm=tm, kc=kc, hout=hout: e.activation(
                        out=hout[:, kc, :], in_=tm[:, 0:T], func=AF.Identity, bias=B_ap[:, kc:kc + 1], scale=A_ap[:, kc:kc + 1]),
                        reads=(tk, "MODPALL"), writes=(("RA", ua),))
                sch.dma(STQ, HT[s][:, :, t0:t0 + T].rearrange("k p t -> p k t"), hout[:, :, :],
                        reads=(("RA", ua),), writes=(("HT", s),))
            else:
                for b in range(T // 128):
                    orow = RA[ua][:, b * 4096:(b + 1) * 4096].bitcast(F32)
                    okey = ("RAo", ua, b)
                    for kc in range(KC):
                        tm, tk = sf()
                        sch.op("dve", lambda e, tm=tm, u=u, kc=kc, r=r, b=b: e.scalar_tensor_tensor(
                            tm[:, 0:128], RBv[:, u, kc, b * 128:(b + 1) * 128], A_ap[:, kc:kc + 1], r[:, b * 128:(b + 1) * 128],
                            ALU.mult, ALU.mult), reads=(("RB", u), rk, "NFIN"), writes=(tk,))
                        if kc % 4 == 0:
                            ps2, pk2 = psb()
                        sch.op("pe", lambda e, ps2=ps2, tm=tm, kc=kc: e.transpose(
                            ps2[:, (kc % 4) * 128:(kc % 4 + 1) * 128], tm[:, 0:128], IDENT[:, :]),
                            reads=(tk, "IDENT"), writes=(pk2,), signal=True)
                        if kc % 4 == 3:
                            copy_op(evac_eng(), orow[:, (kc - 3) * 128:(kc + 1) * 128], ps2[:, :], (pk2,), (okey, ("RA", ua)))
                    sch.dma(STQ, y_out[s][t0 + b * 128:t0 + (b + 1) * 128, :], orow[:, :],
                            reads=(okey,), writes=(("Y", s),))

    def linear_pass(s, src, src_key, kcn, wv, h, col_ranges, n_oc, epilogue, T, tiles=None, in_buf="RA", v_token_major=None,
                    pre_tile=None, oc_order=None):
        S = S_list[s]
        if tiles is None:
            tiles = [(t0, T, t0, T, 0) for t0 in range(0, S, T)]
        if oc_order is None:
            oc_order = list(range(n_oc))

        us = [next_slot(in_buf) for _ in tiles]

        def bufof(ti):
            u = us[ti]
            if in_buf == "RA":
                return RA[u][:, 0:kcn * T].rearrange("p (k t) -> p k t", k=kcn), ("RA", u)
            return RB[:, :].bitcast(BF16)[:, u * 11264:u * 11264 + kcn * T].rearrange("p (k t) -> p k t", k=kcn), ("RB", u)

        def load(ti):
            lo, n, olo, on, coff = tiles[ti]
            buf, bkey = bufof(ti)
            vlo, vhi = max(lo, 0), min(lo + n, S)
            if vlo > lo or vhi < lo + n:
                sch.op("dve", lambda e, buf=buf, n=n: e.memset(buf[:, :, 0:n], 0.0), writes=(bkey,))
            sch.dma("sp", buf[:, :, vlo - lo:vhi - lo], src[:, :, vlo:vhi].rearrange("k p t -> p k t"),
                    reads=(src_key,), writes=(bkey,))
            if pre_tile is not None:
                pre_tile(ti, tiles[ti])

        load(0)
        for ti, (lo, n, olo, on, coff) in enumerate(tiles):
            if ti + 1 < len(tiles):
                load(ti + 1)
            buf, bkey = bufof(ti)
            rd_in = (bkey,)
            for oc in oc_order:
                ps, pk = psb()
                for kc in range(kcn):
                    mm(ps[:, 0:n], wv[:, kc, oc * 128:(oc + 1) * 128], buf[:, kc, 0:n], kc == 0, kc == kcn - 1,
                       reads=tuple(wkeys(h, kc, col_ranges)) + rd_in, writes=(pk,))
                epilogue(oc, ps, pk, (lo, n, olo, on, coff), ti)
            if v_token_major is not None:
                vc0, vn, vdst0 = v_token_major
                for tb in range(n // 128):
                    for c in range(0, vn, 512):
                        cw = min(512, vn - c)
                        ps, pk = psb()
                        for kc in range(kcn):
                            mm(ps[:, 0:cw], buf[:, kc, tb * 128:(tb + 1) * 128], wv[:, kc, vc0 + c:vc0 + c + cw], kc == 0, kc == kcn - 1,
                               reads=tuple(wkeys(h, kc, col_ranges)) + rd_in, writes=(pk,))
                        st, sk = sh()
                        copy_op(evac_eng(), st[:, 0:cw], ps[:, 0:cw], (pk,), (sk,))
                        sch.dma(STQ, VV[s][lo + tb * 128:lo + (tb + 1) * 128, vdst0 + c:vdst0 + c + cw], st[:, 0:cw],
                                reads=(sk,), writes=(("VV", s),))

    def phase_qkv(s, l):
        m = l % 4
        S = S_list[s]
        T = 512
        w = wqkv[m]
        if m == 2:
            nq, nk, nv = 16, 8, 1024
        else:
            nq, nk, nv = 16, 4, 512
        fm_chunks = [("q", i) for i in range(nq)] + [("k", i) for i in range(nk)]
        groups = [fm_chunks[i:i + 12] for i in range(0, len(fm_chunks), 12)]
        vcol0 = (nq + nk) * 128
        plans = []
        for gi, g in enumerate(groups):
            cols = [(((0 if kind == "q" else nq) + i) * 128) for kind, i in g]
            cr = [(cols[0], cols[-1] + 128)]
            vt = None
            if gi == len(groups) - 1 and (len(g) * 128 + nv) * KC <= 24576:
                cr = [(cols[0], cols[-1] + 128 + nv)]
                vt = (len(g) * 128, nv, 0)
            plans.append((g, cr, vt))
        if plans[-1][2] is None:
            plans.append(([], [(vcol0, vcol0 + nv)], (0, nv, 0)))

        def make_epi(g):
            def epi(oc, ps, pk, tinfo, ti):
                lo, n, olo, on, coff = tinfo
                kind, i = g[oc]
                ROPEC, ROPES, rkey = rope_bufs(ti)
                dst = QT[s][i] if kind == "q" else KT[s][i]
                dkey = ("QT", s) if kind == "q" else ("KT", s)
                if m != 0:
                    st, sk = sh()
                    copy_op(evac_eng(), st[:, 0:n], ps[:, 0:n], (pk,), (sk,))
                    sch.dma(STQ, dst[:, lo:lo + n], st[:, 0:n], reads=(sk,), writes=(dkey,))
                    return
                gcol = SMALL[:, 0:1] if kind == "q" else SMALL[:, 1:2]
                sq, sqk = sh()
                sch.op("act", lambda e: e.activation(out=sq[:, 0:n], in_=ps[:, 0:n], func=AF.Square), reads=(pk,), writes=(sqk,))
                ps2, pk2 = psb()
                mm(ps2[:, 0:n], ONESB[:, :], sq[:, 0:n], True, True, reads=(sqk, "ONESB"), writes=(pk2,))
                r, rk = rstd_from_ps(ps2, pk2, n, HD)
                qn, qnk = sf()
                sch.op("dve", lambda e: e.scalar_tensor_tensor(qn[:, 0:n], ps[:, 0:n], gcol, r[:, 0:n], ALU.mult, ALU.mult),
                       reads=(pk, rk, "SMALL_QN", "SMALL_KN"), writes=(qnk,))
                qb, qbk = sh()
                sch.op("act", lambda e: e.activation(out=qb[:, 0:n], in_=qn[:, 0:n], func=AF.Copy), reads=(qnk,), writes=(qbk,))
                ps3, pk3 = psb()
                mm(ps3[:, 0:n], PMB[:, :], qb[:, 0:n], True, True, reads=(qbk, "PMB"), writes=(pk3,))
                t1, t1k = sf()
                sch.op("dve", lambda e: e.tensor_tensor(t1[:, 0:n], qn[:, 0:n], ROPEC[:, 0:n], ALU.mult),
                       reads=(qnk, rkey), writes=(t1k,))
                t2, t2k = sf()
                sch.op("dve", lambda e: e.tensor_tensor(t2[:, 0:n], ps3[:, 0:n], ROPES[:, 0:n], ALU.mult),
                       reads=(pk3, rkey), writes=(t2k,))
                st, sk = sh()
                sch.op("dve", lambda e: e.tensor_tensor(st[:, 0:n], t1[:, 0:n], t2[:, 0:n], ALU.add),
                       reads=(t1k, t2k), writes=(sk,))
                sch.dma(STQ, dst[:, lo:lo + n], st[:, 0:n], reads=(sk,), writes=(dkey,))
            return epi

        wvs = [None] * len(plans)
        pf = hoisted.pop(("qkv", s, l), None)
        if pf is not None and plans[0][1] == [(0, 1536)]:
            hs = [pf[0]] + [next_wb() for _ in plans[1:]]
            wvs[0] = pf[1]
        else:
            hs = [next_wb() for _ in plans]
            wvs[0] = load_w(hs[0], w, 0, KC, plans[0][1])
        for pi, (g, cr, vt) in enumerate(plans):
            if pi + 1 < len(plans):
                wvs[pi + 1] = load_w(hs[pi + 1], w, 0, KC, plans[pi + 1][1])
            pre = None
            if m == 0 and len(g):
                def pre(ti, tinfo):
                    c_, s_, rkey = rope_bufs(ti)
                    sch.dma("sp", c_[:, 0:T], rope_c_d[:, tinfo[0]:tinfo[0] + T], writes=(rkey,))
                    sch.dma("sp", s_[:, 0:T], rope_s_d[:, tinfo[0]:tinfo[0] + T], writes=(rkey,))
            linear_pass(s, HT[s], ("HT", s), KC, wvs[pi], hs[pi], cr, len(g), make_epi(g), T, v_token_major=vt, pre_tile=pre)

    def rope_bufs(ti):
        u = ti % 2
        return RB[:, u * 5632:u * 5632 + 512], RB[:, u * 5632 + 512:u * 5632 + 1024], ("RB", u)

    def phase_attn(s, l):
        m = l % 4
        S = S_list[s]
        NQ = 512
        TABSINGLE = os.environ.get("K_TABSINGLE", "0") == "1"
        LA = int(os.environ.get("K_LA", "3"))
        nblk = S // 128
        M = 2 if m == 2 else 1
        DV = 2 if m == 2 else 1
        G = 2 if m == 2 else 4
        table = {0: None, 1: (tb_d, TB_CEN, TB_L), 2: (tc_d, TC_CEN, TC_L), 3: (td_d, TD_CEN, TD_L)}[m]
        KB16 = WB[0]
        VB16 = WB[1]
        na = DV + 1
        if m == 2:
            acc_sets = [0]
            rot_lo = na
        else:
            acc_sets = [0, na]
            rot_lo = 2 * na
        allw0 = [("WB", 0, i) for i in range(11)]
        allw1 = [("WB", 1, i) for i in range(11)]
        U0 = RA[0][:, 0:2048].bitcast(F32)

        def kv_views(kvh):
            par = kvh % 2
            kview = KB16[:, par * M * S:(par + 1) * M * S].rearrange("p (m t) -> p m t", m=M)
            vview = VB16[:, par * nblk * DV * 128:(par + 1) * nblk * DV * 128].rearrange("p (b d) -> p b d", b=nblk)
            return kview, vview, ("WB", 0, "att", par), ("WB", 1, "att", par)

        def load_kv(kvh):
            kview, vview, kkey, vkey = kv_views(kvh)
            for mm_ in range(M):
                sch.dma("sp", kview[:, mm_, :], KT[s][kvh * M + mm_], reads=(("KT", s),), writes=tuple([kkey] + allw0))
            for b0 in range(0, nblk, 8):
                b1 = min(nblk, b0 + 8)
                sch.dma("sp", vview[:, b0:b1, :],
                        VV[s][b0 * 128:b1 * 128, kvh * DV * 128:(kvh + 1) * DV * 128].rearrange("(b p) d -> p b d", p=128),
                        reads=(("VV", s),), writes=tuple([vkey] + allw1))

        items = [(kvh, g, qi, mm_) for kvh in range(4) for g in range(G) for qi in range(S // NQ) for mm_ in range(M)]

        def qslot(ii):
            return ii % 2

        def load_q(ii):
            kvh, g, qi, mm_ = items[ii]
            h = kvh * G + g
            sl = qslot(ii)
            sch.dma("sp", QTILE[:, sl * 512:(sl + 1) * 512], QT[s][h * M + mm_][:, qi * NQ:(qi + 1) * NQ],
                    reads=(("QT", s),), writes=(("QTILE", sl),))

        def tab_view(h):
            tab, cen, L = table
            if m == 2 or TABSINGLE:
                return RB[:, 0:L], (("RB", 0), ("RB", 1))
            u = h % 2
            return RB[:, u * 5632:u * 5632 + L], (("RB", u),)

        def load_tab(h):
            tab, cen, L = table
            tv, tk = tab_view(h)
            sch.dma("sp", tv, tab[h], writes=tk)

        load_kv(0)
        if table is not None:
            load_tab(0)
        load_q(0)
        pass_i = 0
        for ii, (kvh, g, qi, mm_) in enumerate(items):
            h = kvh * G + g
            kview, vview, kkey, vkey = kv_views(kvh)
            first_of_kvh = (g == 0 and qi == 0 and mm_ == 0)
            first_of_head = (qi == 0 and mm_ == 0)
            if first_of_kvh and kvh + 1 < 4:
                load_kv(kvh + 1)
            if first_of_head and table is not None and h + 1 < 4 * G and m != 2 and not TABSINGLE:
                load_tab(h + 1)
            if first_of_head and table is not None and (m == 2 or TABSINGLE) and h > 0:
                load_tab(h)
            if ii + 1 < len(items):
                load_q(ii + 1)
            q0 = qi * NQ
            qb0 = q0 // 128
            if m == 1:
                kbs = [kb for kb in range(qb0 - 1, qb0 + 5) if 0 <= kb < nblk]
            elif m == 3:
                kbs = [kb for kb in range(qb0 - 8, qb0 + 12) if 0 <= kb < nblk]
            else:
                kbs = list(range(nblk))
            sl = qslot(ii)
            qt_, qk_ = QTILE[:, sl * 512:(sl + 1) * 512], ("QTILE", sl)
            base = acc_sets[pass_i % len(acc_sets)]
            pass_i += 1
            accO = [(PS[base + dv], ("PS", base + dv)) for dv in range(DV)]
            accZ = (PS[base + DV], ("PS", base + DV))
            if table is not None:
                tv, tkeys = tab_view(h)
                cen = table[1]
            pt = {}

            def stage1(idx):
                kb = kbs[idx]
                ps, pk = psb(rot_lo, 8)
                mm(ps[:, 0:NQ], kview[:, mm_, kb * 128:(kb + 1) * 128], qt_[:, 0:NQ], True, True, reads=(kkey, qk_), writes=(pk,))
                p_, pkk = sh()
                if table is None:
                    sch.op("act", lambda e, p_=p_, ps=ps: e.activation(out=p_[:, 0:NQ], in_=ps[:, 0:NQ], func=AF.Exp, scale=SCALE),
                           reads=(pk,), writes=(pkk,))
                else:
                    e_, ek = sf()
                    sch.op("act", lambda e, e_=e_, ps=ps: e.activation(out=e_[:, 0:NQ], in_=ps[:, 0:NQ], func=AF.Exp, scale=SCALE),
                           reads=(pk,), writes=(ek,))
                    c0 = cen - (kb * 128 - q0)
                    sch.op("dve", lambda e, p_=p_, e_=e_, c0=c0, tv=tv: e.tensor_tensor(p_[:, 0:NQ], e_[:, 0:NQ], tv[:, c0:c0 + NQ], ALU.mult),
                           reads=(ek,) + tkeys, writes=(pkk,))
                pt[idx] = (p_, pkk)

            def stage2(idx):
                kb = kbs[idx]
                p_, pkk = pt.pop(idx)
                first, last = idx == 0, idx == len(kbs) - 1
                for dv in range(DV):
                    mm(accO[dv][0][:, 0:NQ], vview[:, kb, dv * 128:(dv + 1) * 128], p_[:, 0:NQ], first, last,
                       reads=(vkey, pkk), writes=(accO[dv][1],))
                mm(accZ[0][:, 0:NQ], ONESB[:, :], p_[:, 0:NQ], first, last, reads=("ONESB", pkk), writes=(accZ[1],), force=True)

            nk = len(kbs)
            for j in range(nk + LA):
                if j < nk:
                    stage1(j)
                if j >= LA:
                    stage2(j - LA)
            z, zk = sf()
            zp, zpk = accZ
            if m == 1:
                sch.op("dve", lambda e, z=z, zp=zp, h=h: e.tensor_scalar(z[:, 0:NQ], zp[:, 0:NQ], SMALL[:, 8 + h:9 + h], None, op0=ALU.add),
                       reads=(zpk, "SMALL_ES"), writes=(zk,))
                sch.op("dve", lambda e, z=z: e.reciprocal(z[:, 0:NQ], z[:, 0:NQ]), reads=(zk,), writes=(zk,))
            else:
                sch.op("dve", lambda e, z=z, zp=zp: e.reciprocal(z[:, 0:NQ], zp[:, 0:NQ]), reads=(zpk,), writes=(zk,))
            if m != 2:
                o_, ok = sh()
                op_, opk = accO[0]
                sch.op("dve", lambda e, o_=o_, op_=op_, z=z: e.tensor_tensor(o_[:, 0:NQ], op_[:, 0:NQ], z[:, 0:NQ], ALU.mult),
                       reads=(opk, zk), writes=(ok,))
                sch.dma(STQ, OT[s][h][:, q0:q0 + NQ], o_[:, 0:NQ], reads=(ok,), writes=(("OT", s),))
            elif mm_ == 0:
                for dv in range(DV):
                    sch.op("dve", lambda e, dv=dv, a=accO[dv][0], z=z: e.tensor_tensor(U0[:, dv * 512:dv * 512 + NQ], a[:, 0:NQ], z[:, 0:NQ], ALU.mult),
                           reads=(accO[dv][1], zk), writes=(("RA", 0),))
            else:
                psn, pnk = psb(rot_lo, 8)
                for dv in range(DV):
                    v_, vk = sf()
                    sch.op("dve", lambda e, v_=v_, a=accO[dv][0], z=z: e.tensor_tensor(v_[:, 0:NQ], a[:, 0:NQ], z[:, 0:NQ], ALU.mult),
                           reads=(accO[dv][1], zk), writes=(vk,))
                    sch.op("dve", lambda e, v_=v_, dv=dv: e.scalar_tensor_tensor(
                        U0[:, dv * 512:dv * 512 + NQ], v_[:, 0:NQ], SMALL[:, 4:5], U0[:, dv * 512:dv * 512 + NQ], ALU.mult, ALU.add),
                        reads=(vk, "SMALL_NL", ("RA", 0)), writes=(("RA", 0),))
                    q_, qk2 = sh()
                    sch.op("act", lambda e, q_=q_, dv=dv: e.activation(out=q_[:, 0:NQ], in_=U0[:, dv * 512:dv * 512 + NQ], func=AF.Square),
                           reads=(("RA", 0),), writes=(qk2,))
                    mm(psn[:, 0:NQ], ONESB[:, :], q_[:, 0:NQ], dv == 0, dv == DV - 1, reads=(qk2, "ONESB"), writes=(pnk,), force=True)
                r, rk = rstd_from_ps(psn, pnk, NQ, 2 * HD)
                for dv in range(DV):
                    o_, ok = sh()
                    sch.op("dve", lambda e, o_=o_, dv=dv, r=r: e.scalar_tensor_tensor(
                        o_[:, 0:NQ], U0[:, dv * 512:dv * 512 + NQ], SMALL[:, 2 + dv:3 + dv], r[:, 0:NQ], ALU.mult, ALU.mult),
                        reads=(("RA", 0), rk, "SMALL_SUBW"), writes=(ok,))
                    sch.dma(STQ, OT[s][h * 2 + dv][:, q0:q0 + NQ], o_[:, 0:NQ], reads=(ok,), writes=(("OT", s),))

    def phase_resid(s, src, src_key, kcn, w2d, gate_ap, T, oc_per, in_buf):
        S = S_list[s]
        n_sub = KC // oc_per
        hs = [next_wb() for _ in range(n_sub)]
        wvs = [None] * n_sub
        wvs[0] = load_w(hs[0], w2d, 0, kcn, [(0, oc_per * 128)])
        for si in range(n_sub):
            if si + 1 < n_sub:
                wvs[si + 1] = load_w(hs[si + 1], w2d, 0, kcn, [((si + 1) * oc_per * 128, (si + 2) * oc_per * 128)])
            cr = [(si * oc_per * 128, (si + 1) * oc_per * 128)]

            def epi(oc, ps, pk, tinfo, ti, si=si):
                lo, n, olo, on, coff = tinfo
                ocg = si * oc_per + oc
                xc, xk = sf()
                sch.dma("sp", xc[:, 0:n], XT[s][ocg, :, lo:lo + n], reads=(("XT", s, ocg),), writes=(xk,))
                sch.op("dve", lambda e: e.scalar_tensor_tensor(xc[:, 0:n], ps[:, 0:n], gate_ap[:, ocg:ocg + 1], xc[:, 0:n], ALU.mult, ALU.add),
                       reads=(pk, xk, "MODPALL"), writes=(xk,))
                sch.dma(STQ, XT[s][ocg, :, lo:lo + n], xc[:, 0:n], reads=(xk,), writes=(("XT", s, ocg),))

            linear_pass(s, src, src_key, kcn, wvs[si], hs[si], cr, oc_per, epi, T, in_buf=in_buf)

    def phase_ffn_up(s, l):
        S = S_list[s]
        T = 512
        step = T - 2
        tiles = []
        t0 = 0
        while t0 < S:
            on = min(step, S - t0)
            tiles.append((t0 - 1, on + 2, t0, on, 1))
            t0 += step
        subs = [list(range(i, min(i + 6, FC))) for i in range(0, FC, 6)]
        wvs = [None] * len(subs)

        def crs(fcs):
            return [(fcs[0] * 128, (fcs[-1] + 1) * 128), (DFF + fcs[0] * 128, DFF + (fcs[-1] + 1) * 128)]

        pf = hoisted.pop(("ffn", s, l), None)
        if pf is not None:
            hs = [pf[0]] + [next_wb() for _ in subs[1:]]
            wvs[0] = pf[1]
        else:
            hs = [next_wb() for _ in subs]
            wvs[0] = load_w(hs[0], w_up[l], 0, KC, crs(subs[0]))
        for si, fcs in enumerate(subs):
            if si + 1 < len(subs):
                wvs[si + 1] = load_w(hs[si + 1], w_up[l], 0, KC, crs(subs[si + 1]))
            nf = len(fcs)
            pend = {}

            def epi(oc, ps, pk, tinfo, ti, fcs=fcs, nf=nf, pend=pend):
                lo, n, olo, on, coff = tinfo
                if oc < nf:
                    fc = fcs[oc]
                    w0 = CONVW[:, (l * 3 + 0) * FC + fc:(l * 3 + 0) * FC + fc + 1]
                    w1 = CONVW[:, (l * 3 + 1) * FC + fc:(l * 3 + 1) * FC + fc + 1]
                    w2 = CONVW[:, (l * 3 + 2) * FC + fc:(l * 3 + 2) * FC + fc + 1]
                    cb = CONVB[:, l * FC + fc:l * FC + fc + 1]
                    u_, uk = sf()
                    ck = (("CONVW", l, 0), ("CONVW", l, 1), ("CONVW", l, 2), ("CONVB", l))
                    sch.op("dve", lambda e: e.tensor_scalar(u_[:, 0:on], ps[:, 0:on], w0, None, op0=ALU.mult), reads=(pk,) + ck, writes=(uk,))
                    sch.op("dve", lambda e: e.scalar_tensor_tensor(u_[:, 0:on], ps[:, 1:1 + on], w1, u_[:, 0:on], ALU.mult, ALU.add),
                           reads=(pk, uk) + ck, writes=(uk,))
                    sch.op("dve", lambda e: e.scalar_tensor_tensor(u_[:, 0:on], ps[:, 2:2 + on], w2, u_[:, 0:on], ALU.mult, ALU.add),
                           reads=(pk, uk) + ck, writes=(uk,))
                    sch.op("act", lambda e: e.activation(out=u_[:, 0:on], in_=u_[:, 0:on], func=AF.Gelu, bias=cb), reads=(uk,) + ck, writes=(uk,))
                    pend[oc] = (u_, uk)
                else:
                    fc = fcs[oc - nf]
                    u_, uk = pend.pop(oc - nf)
                    g_, gk = sh()
                    sch.op("dve", lambda e: e.tensor_tensor(g_[:, 0:on], u_[:, 0:on], ps[:, 1:1 + on], ALU.mult), reads=(uk, pk), writes=(gk,))
                    sch.dma(STQ, GT[s][fc][:, olo:olo + on], g_[:, 0:on], reads=(gk,), writes=(("GT", s),))

            order = [o for f in range(nf) for o in (f, nf + f)]
            linear_pass(s, HT[s], ("HT", s), KC, wvs[si], hs[si], crs(fcs), 2 * nf, epi, T, tiles=tiles, oc_order=order)

    def linear_pass_ab(s, wv, h, col_ranges, nf, epi, T, tiles):
        S = S_list[s]
        for ti, (lo, n, olo, on, coff) in enumerate(tiles):
            u = ti % 2
            buf = RA[u][:, 0:KC * T].rearrange("p (k t) -> p k t", k=KC)
            bkey = ("RA", u)
            vlo, vhi = max(lo, 0), min(lo + n, S)
            if vlo > lo or vhi < lo + n:
                sch.op("dve", lambda e, buf=buf, n=n: e.memset(buf[:, :, 0:n], 0.0), writes=(bkey,))
            sch.dma("sp", buf[:, :, vlo - lo:vhi - lo], HT[s][:, :, vlo:vhi].rearrange("k p t -> p k t"),
                    reads=(("HT", s),), writes=(bkey,))
            for f in range(nf):
                for oc in (f, nf + f):
                    ps, pk = psb()
                    for kc in range(KC):
                        mm(ps[:, 0:n], wv[:, kc, oc * 128:(oc + 1) * 128], buf[:, kc, 0:n], kc == 0, kc == KC - 1,
                           reads=tuple(wkeys(h, kc, col_ranges)) + (bkey,), writes=(pk,))
                    epi(oc, ps, pk, (lo, n, olo, on, coff))

    hoisted = {}

    def hoist_qkv(s, l):
        h = next_wb()
        hoisted[("qkv", s, l)] = (h, load_w(h, wqkv[l % 4], 0, KC, [(0, 1536)]))

    def hoist_ffn(s, l):
        h = next_wb()
        hoisted[("ffn", s, l)] = (h, load_w(h, w_up[l], 0, KC, [(0, 768), (DFF, DFF + 768)]))

    for s in range(NSEQ):
        phase_transpose_in(s)
    for l in range(depth):
        for s in range(NSEQ):
            hoist_qkv(s, l)
            phase_norm(s, modp(s, l, 0), modp(s, l, 1))
            phase_qkv(s, l)
            phase_attn(s, l)
            phase_resid(s, OT[s], ("OT", s), KC, wo[l % 4], modp(s, l, 2), 512, 8, "RA")
            hoist_ffn(s, l)
            phase_norm(s, modp(s, l, 3), modp(s, l, 4))
            phase_ffn_up(s, l)
            phase_resid(s, GT[s], ("GT", s), FC, w_down[l], modp(s, l, 5), 256, 4, "RB")
    for s in range(NSEQ):
        phase_norm(s, NFIN, None, final=True)

    sch.finish()
    sch.emit()
    es.close()
    return nc


_PROG_CACHE = {}


def _get_prog(S_list, depth=DEPTH):
    key = (tuple(S_list), depth)
    if key not in _PROG_CACHE:
        _PROG_CACHE[key] = build_program(list(S_list), depth)
    return _PROG_CACHE[key]


def run_cores(xs_per_core, cs_per_core, weights, S_list, depth=DEPTH):
    nc = _get_prog(S_list, depth)
    consts = _const_tables()
    f = lambda a: np.ascontiguousarray(np.asarray(a, dtype=np.float32))
    shared = {
        "norm_attn": f(weights["norm_attn"]).reshape(DEPTH, KC, 128),
        "norm_ffn": f(weights["norm_ffn"]).reshape(DEPTH, KC, 128),
        "w_ada": f(weights["w_ada"]),
        "b_ada": f(weights["b_ada"]).reshape(DEPTH, 96, 128),
        "a_wqkv": f(weights["a_wqkv"])[0], "b_wqkv": f(weights["b_wqkv"])[0],
        "c_wqkv": f(weights["c_wqkv"])[0], "d_wqkv": f(weights["d_wqkv"])[0],
        "a_wo": f(weights["a_wo"])[0], "b_wo": f(weights["b_wo"])[0],
        "c_wo": f(weights["c_wo"])[0], "d_wo": f(weights["d_wo"])[0],
        "a_q_norm": f(weights["a_q_norm"]).reshape(1, 128),
        "a_k_norm": f(weights["a_k_norm"]).reshape(1, 128),
        "b_sink": f(weights["b_sink"]).reshape(1, 16),
        "c_lambda": f(weights["c_lambda"]).reshape(4, 128),
        "c_subln": f(weights["c_subln"]).reshape(2, 128),
        "ffn_w_up": f(weights["ffn_w_up"]),
        "ffn_conv_w": f(weights["ffn_conv_w"]).reshape(DEPTH, 3, FC, 128),
        "ffn_conv_b": f(weights["ffn_conv_b"]).reshape(DEPTH, FC, 128),
        "ffn_w_down": f(weights["ffn_w_down"]),
        "norm_final": f(weights["norm_final"]).reshape(KC, 128),
    }
    shared.update(consts)
    in_maps = []
    for c in range(len(xs_per_core)):
        m = dict(shared)
        for s in range(len(S_list)):
            m["x%d" % s] = f(xs_per_core[c][s])
            m["c%d" % s] = f(cs_per_core[c][s]).reshape(KC, 128)
        in_maps.append(m)
    res = run_bass_kernel_spmd(nc, in_maps, core_ids=list(range(len(in_maps))))
    return [[np.asarray(r["y%d" % s]) for s in range(len(S_list))] for r in res.results]


def kernel(x_prompt, x_sample, c_prompt, c_sample, **weights):
    x_prompt = np.asarray(x_prompt, dtype=np.float32)
    x_sample = np.asarray(x_sample, dtype=np.float32)
    c_prompt = np.asarray(c_prompt, dtype=np.float32)
    c_sample = np.asarray(c_sample, dtype=np.float32)
    nb_p = x_prompt.shape[0]
    xs, cs = [], []
    for c in range(N_CORES):
        xs.append([x_sample[c], x_prompt[c % nb_p]])
        cs.append([c_sample[c], c_prompt[c % nb_p]])
    outs = run_cores(xs, cs, weights, [x_sample.shape[1], x_prompt.shape[1]])
    y_sample = np.stack([outs[c][0] for c in range(N_CORES)], axis=0)
    y_prompt = np.stack([outs[c][1] for c in range(nb_p)], axis=0)
    return (y_prompt.astype(np.float32), y_sample.astype(np.float32))
```

```python
import math
import os
from contextlib import ExitStack

import numpy as np
import concourse.bass as bass
import concourse.mybir as mybir
from concourse.bass_utils import run_bass_kernel_spmd

F32 = mybir.dt.float32
BF16 = mybir.dt.bfloat16
AF = mybir.ActivationFunctionType
ALU = mybir.AluOpType

D = 2048
KC = 16
DFF = 5504
FC = 43
HD = 128
EPS = 1e-6
DEPTH = 4
SCALE = HD ** -0.5
N_CORES = 8


STQ = os.environ.get("K_STQ", "pool")
NODEDUP = os.environ.get("K_NODEDUP", "0") == "1"


class Sched:
    ENGS = ("pe", "act", "dve", "pool", "sp")

    def __init__(self, nc, es, n_dma_sems=56):
        self.nc = nc
        self.ops = {e: [] for e in self.ENGS}
        self.esem = {e: es.enter_context(nc.semaphore("s_" + e)) for e in ("pe", "act", "dve", "pool")}
        self.ecount = {e: 0 for e in self.ENGS}
        self.dsem = [es.enter_context(nc.semaphore("d%d" % i)) for i in range(n_dma_sems)]
        self.dcount = [0] * n_dma_sems
        self.dnext = 0
        self.dnext_pool = 0
        self.waited = {e: {} for e in self.ENGS}
        self.lastw = {}
        self.readers = {}

    def _deps(self, reads, writes):
        deps = {}

        def add(k, v):
            if deps.get(k, 0) < v:
                deps[k] = v

        for b in reads:
            t = self.lastw.get(b)
            if t:
                add(*t)
        for b in writes:
            t = self.lastw.get(b)
            if t:
                add(*t)
            for k, v in self.readers.get(b, {}).items():
                add(k, v)
        return deps

    def _reduce(self, eng, deps):
        out = []
        w = self.waited[eng]
        for k, v in deps.items():
            if k == ("e", "pe") and eng == "pe":
                continue
            if w.get(k, 0) < v or (NODEDUP and k[0] == "d" and eng in ("dve", "act")):
                w[k] = max(v, w.get(k, 0))
                out.append((k, v))
        return out

    def _commit(self, tok, reads, writes):
        for b in writes:
            self.lastw[b] = tok
            self.readers[b] = {}
        for b in reads:
            r = self.readers.setdefault(b, {})
            if r.get(tok[0], 0) < tok[1]:
                r[tok[0]] = tok[1]

    def op(self, eng, fn, reads=(), writes=(), signal=True):
        waits = self._reduce(eng, self._deps(reads, writes))
        if signal:
            self.ecount[eng] += 1
            tok = (("e", eng), self.ecount[eng])
        else:
            tok = (("e", eng), self.ecount[eng] + 1)
        self.ops[eng].append((waits, fn, True if signal else None))
        self._commit(tok, reads, writes)
        return tok

    def dma(self, q, out, in_, reads=(), writes=()):
        NP = 24
        if q == "pool":
            i = self.dnext_pool
            self.dnext_pool = (i + 1) % NP
        else:
            i = NP + self.dnext
            self.dnext = (self.dnext + 1) % (len(self.dsem) - NP)
        deps = self._deps(reads, writes)
        if self.dcount[i]:
            k = ("d", i)
            if deps.get(k, 0) < self.dcount[i]:
                deps[k] = self.dcount[i]
        waits = self._reduce(q, deps)
        self.dcount[i] += 16
        tok = (("d", i), self.dcount[i])
        self.ops[q].append((waits, (lambda e, o=out, s=in_: e.dma_start(out=o, in_=s)), ("d", i)))
        self._commit(tok, reads, writes)
        return tok

    def _sem(self, k):
        return self.esem[k[1]] if k[0] == "e" else self.dsem[k[1]]

    def finish(self):
        deps = {}
        for i, c in enumerate(self.dcount):
            if c:
                deps[("d", i)] = c
        for e in ("pe", "act", "dve", "pool"):
            if self.ecount[e]:
                deps[("e", e)] = self.ecount[e]
        waits = self._reduce("sp", deps)
        self.ops["sp"].append((waits, None, None))

    def _emit(self, name, e):
        for waits, fn, sig in self.ops[name]:
            for k, v in waits:
                e.wait_ge(self._sem(k), v)
            if fn is None:
                continue
            ins = fn(e)
            if sig is True:
                ins.then_inc(self.esem[name], 1)
            elif sig is not None:
                ins.then_inc(self.dsem[sig[1]], 16)

    def emit(self):
        with self.nc.Block() as block:
            @block.tensor
            def _(e):
                self._emit("pe", e)

            @block.scalar
            def _(e):
                self._emit("act", e)

            @block.vector
            def _(e):
                self._emit("dve", e)

            @block.gpsimd
            def _(e):
                self._emit("pool", e)

            @block.sync
            def _(e):
                self._emit("sp", e)


def _alibi_slopes(n):
    return 2.0 ** (-8.0 * np.arange(1, n + 1, dtype=np.float64) / n)


TB_CEN, TB_L = 512, 1152
TD_CEN, TD_L = 1408, 2944
TC_CEN, TC_L = 3968, 8064


def _toeplitz(fvals_fn, cen, L):
    i = np.arange(128)[:, None]
    c = np.arange(L)[None, :]
    d = i - c + cen
    return fvals_fn(d)


def _const_tables():
    sl16 = _alibi_slopes(16)
    sl8 = _alibi_slopes(8)
    tb = np.zeros((16, 128, TB_L), np.float32)
    td = np.zeros((16, 128, TD_L), np.float32)
    tc = np.zeros((8, 128, TC_L), np.float32)

    def mult_d(d):
        a = np.abs(d)
        m = (a <= 64).astype(np.float64)
        m += ((a <= 256) & (a % 4 == 0)).astype(np.float64)
        m += ((a <= 1024) & (a % 16 == 0)).astype(np.float64)
        return m

    for h in range(16):
        tb[h] = _toeplitz(lambda d: (np.abs(d) <= 128) * np.exp(-sl16[h] * np.abs(d)), TB_CEN, TB_L)
        td[h] = _toeplitz(lambda d: mult_d(d) * np.exp(-sl16[h] * np.abs(d)), TD_CEN, TD_L)
    for h in range(8):
        tc[h] = _toeplitz(lambda d: np.exp(-sl8[h] * np.abs(d)), TC_CEN, TC_L)
    t = np.arange(4096, dtype=np.float64)
    row, col = np.floor(t / 64), t % 64
    inv = 10000.0 ** (-np.arange(32, dtype=np.float64) / 32)
    rc = np.zeros((128, 4096), np.float32)
    rs = np.zeros((128, 4096), np.float32)
    pm = np.zeros((128, 128), np.float32)
    for dd in range(128):
        a, m, f = dd // 64, (dd // 32) % 2, dd % 32
        ang = (row if a == 0 else col) * inv[f]
        rc[dd] = np.cos(ang)
        rs[dd] = np.sin(ang) * (-1.0 if m == 0 else 1.0)
        src = a * 64 + (1 - m) * 32 + f
        pm[src, dd] = 1.0
    ident = np.eye(128, dtype=np.float32)
    return dict(tb=tb, td=td, tc=tc, rope_c=rc, rope_s=rs, pm=pm, ident=ident)


def _lambda_init(layer):
    return 0.8 - 0.6 * math.exp(-0.3 * layer)


def build_program(S_list, depth=DEPTH):
    nc = bass.Bass("TRN2", target_bir_lowering=False)
    es = ExitStack()
    NSEQ = len(S_list)

    def din(name, shape, dt=F32):
        return nc.dram_tensor(name, list(shape), dt, kind="ExternalInput").ap()

    def dscr(name, shape, dt):
        return nc.dram_tensor(name, list(shape), dt, kind="Internal").ap()

    x_in = [din("x%d" % s, [S, D]) for s, S in enumerate(S_list)]
    c_in = [din("c%d" % s, [KC, 128]) for s in range(NSEQ)]
    y_out = [nc.dram_tensor("y%d" % s, [S, D], F32, kind="ExternalOutput").ap() for s, S in enumerate(S_list)]
    norm_attn = din("norm_attn", [DEPTH, KC, 128])
    norm_ffn = din("norm_ffn", [DEPTH, KC, 128])
    w_ada = din("w_ada", [DEPTH, D, 6 * D])
    b_ada = din("b_ada", [DEPTH, 96, 128])
    wqkv = [din("a_wqkv", [D, 3072]), din("b_wqkv", [D, 3072]), din("c_wqkv", [D, 4096]), din("d_wqkv", [D, 3072])]
    wo = [din("a_wo", [D, D]), din("b_wo", [D, D]), din("c_wo", [D, D]), din("d_wo", [D, D])]
    a_q_norm = din("a_q_norm", [1, 128])
    a_k_norm = din("a_k_norm", [1, 128])
    b_sink = din("b_sink", [1, 16])
    c_lambda = din("c_lambda", [4, 128])
    c_subln = din("c_subln", [2, 128])
    w_up = din("ffn_w_up", [DEPTH, D, 2 * DFF])
    conv_w = din("ffn_conv_w", [DEPTH, 3, FC, 128])
    conv_b = din("ffn_conv_b", [DEPTH, FC, 128])
    w_down = din("ffn_w_down", [DEPTH, DFF, D])
    norm_final = din("norm_final", [KC, 128])
    tb_d = din("tb", [16, 128, TB_L])
    td_d = din("td", [16, 128, TD_L])
    tc_d = din("tc", [8, 128, TC_L])
    rope_c_d = din("rope_c", [128, 4096])
    rope_s_d = din("rope_s", [128, 4096])
    pm_d = din("pm", [128, 128])
    ident_d = din("ident", [128, 128])

    XT = [dscr("XT%d" % s, [KC, 128, S], F32) for s, S in enumerate(S_list)]
    HT = [dscr("HT%d" % s, [KC, 128, S], BF16) for s, S in enumerate(S_list)]
    QT = [dscr("QT%d" % s, [16, 128, S], BF16) for s, S in enumerate(S_list)]
    KT = [dscr("KT%d" % s, [8, 128, S], BF16) for s, S in enumerate(S_list)]
    VV = [dscr("VV%d" % s, [S, 1024], BF16) for s, S in enumerate(S_list)]
    OT = [dscr("OT%d" % s, [16, 128, S], BF16) for s, S in enumerate(S_list)]
    GT = [dscr("GT%d" % s, [FC, 128, S], BF16) for s, S in enumerate(S_list)]

    def sb(name, shape, dt):
        return es.enter_context(nc.sbuf_tensor(name, list(shape), dt))

    WB = [sb("WB0", [128, 24576], BF16), sb("WB1", [128, 24576], BF16)]
    RA = [sb("RA0", [128, 8192], BF16), sb("RA1", [128, 8192], BF16)]
    RB = sb("RB", [128, 11264], F32)
    NF32 = 6
    NB16 = 6
    RSTD = [sb("RSTD%d" % i, [128, 256], F32) for i in range(2)]
    QTILE = sb("QTILE", [128, 1024], BF16)
    SF = [sb("SF%d" % i, [128, 512], F32) for i in range(NF32)]
    SH = [sb("SH%d" % i, [128, 512], BF16) for i in range(NB16)]
    IDENT = sb("IDENT", [128, 128], F32)
    ONESB = sb("ONESB", [128, 128], BF16)
    ONESF = sb("ONESF", [128, 128], F32)
    PMB = sb("PMB", [128, 128], BF16)
    PMF = sb("PMF", [128, 128], F32)
    MODP = sb("MODP", [128, NSEQ * DEPTH * 6 * KC], F32)
    MODRAW = sb("MODRAW", [128, 96 * NSEQ], F32)
    BADA = sb("BADA", [128, 96], F32)
    NAT = sb("NAT", [128, 2 * KC], F32)
    SCT = sb("SCT", [128, KC * NSEQ], F32)
    CT = sb("CT", [128, KC * NSEQ], F32)
    CONVW = sb("CONVW", [128, DEPTH * 3 * FC], F32)
    CONVB = sb("CONVB", [128, DEPTH * FC], F32)
    NFIN = sb("NFIN", [128, KC], F32)
    SMALL = sb("SMALL", [128, 64], F32)
    ROWS = sb("ROWS", [128, 128], F32)
    PS = [es.enter_context(nc.psum_tensor("PS%d" % i, [128, 512], F32)) for i in range(8)]

    sch = Sched(nc, es)
    ctr = {"sf": 0, "sh": 0, "ps": 0, "ev": 0}

    def sf():
        i = ctr["sf"]
        ctr["sf"] = (i + 1) % NF32
        return SF[i], ("SF", i)

    def sh():
        i = ctr["sh"]
        ctr["sh"] = (i + 1) % NB16
        return SH[i], ("SH", i)

    def psb(lo=0, hi=8):
        i = ctr["ps"]
        if i < lo or i >= hi:
            i = lo
        ctr["ps"] = i + 1 if i + 1 < hi else lo
        return PS[i], ("PS", i)

    slot_ctr = {"RA": 0, "RB": 0}

    def next_slot(name):
        u = slot_ctr[name]
        slot_ctr[name] ^= 1
        return u

    def evac_eng():
        ctr["ev"] ^= 1
        return "act" if ctr["ev"] else "dve"

    def copy_op(eng, out, in_, reads, writes):
        if eng == "act":
            return sch.op("act", lambda e: e.activation(out=out, in_=in_, func=AF.Copy), reads, writes)
        return sch.op(eng, lambda e: e.tensor_copy(out, in_), reads, writes)

    def mm(out, lhsT, rhs, start, stop, reads, writes, force=False):
        return sch.op("pe", lambda e: e.matmul(out, lhsT, rhs, start=start, stop=stop), reads, writes, signal=(stop or force))

    sch.dma("sp", IDENT[:], ident_d, writes=("IDENT",))
    sch.dma("sp", PMF[:], pm_d, writes=("PMF",))
    sch.op("dve", lambda e: e.tensor_copy(PMB[:], PMF[:]), reads=("PMF",), writes=("PMB",))
    sch.op("dve", lambda e: e.memset(ONESB[:], 1.0), writes=("ONESB",))
    sch.op("dve", lambda e: e.memset(ONESF[:], 1.0), writes=("ONESF",))

    def load_T(src_rows, nrows, dst, dst_key):
        sch.dma("sp", ROWS[0:nrows, :], src_rows, writes=("ROWS",))
        ps, pk = psb()
        sch.op("pe", lambda e: e.transpose(ps[:, 0:nrows], ROWS[0:nrows, :], IDENT[0:nrows, 0:nrows]),
               reads=("ROWS", "IDENT"), writes=(pk,))
        sch.op("dve", lambda e: e.tensor_copy(dst, ps[:, 0:nrows]), reads=(pk,), writes=(dst_key,))

    load_T(norm_final, KC, NFIN[:, :], "NFIN")
    load_T(a_q_norm, 1, SMALL[:, 0:1], "SMALL_QN")
    load_T(a_k_norm, 1, SMALL[:, 1:2], "SMALL_KN")
    load_T(c_subln, 2, SMALL[:, 2:4], "SMALL_SUBW")
    sch.op("dve", lambda e: e.tensor_scalar(SMALL[:, 2:4], SMALL[:, 2:4], 1.0 - _lambda_init(2), None, op0=ALU.mult),
           reads=("SMALL_SUBW",), writes=("SMALL_SUBW",))
    load_T(c_lambda, 4, SMALL[:, 24:28], "SMALL_LT")
    sch.op("dve", lambda e: e.tensor_tensor(SMALL[:, 28:29], SMALL[:, 24:25], SMALL[:, 25:26], ALU.mult),
           reads=("SMALL_LT",), writes=("SMALL_LP0",))
    sch.op("dve", lambda e: e.tensor_tensor(SMALL[:, 29:30], SMALL[:, 26:27], SMALL[:, 27:28], ALU.mult),
           reads=("SMALL_LT",), writes=("SMALL_LP1",))
    ps, pk = psb()
    mm(ps[:, 0:2], ONESF[:], SMALL[:, 28:30], True, True, reads=("ONESF", "SMALL_LP0", "SMALL_LP1"), writes=(pk,))
    sch.op("act", lambda e, ps=ps: e.activation(out=SMALL[:, 30:32], in_=ps[:, 0:2], func=AF.Exp),
           reads=(pk,), writes=("SMALL_LE",))
    sch.op("dve", lambda e: e.tensor_tensor(SMALL[:, 4:5], SMALL[:, 31:32], SMALL[:, 30:31], ALU.subtract),
           reads=("SMALL_LE",), writes=("SMALL_NL",))
    sch.op("dve", lambda e: e.tensor_scalar(SMALL[:, 4:5], SMALL[:, 4:5], -_lambda_init(2), None, op0=ALU.add),
           reads=("SMALL_NL",), writes=("SMALL_NL",))
    sch.dma("sp", SMALL[:, 8:24], b_sink.partition_broadcast(128), writes=("SMALL_ES",))
    sch.op("act", lambda e: e.activation(out=SMALL[:, 8:24], in_=SMALL[:, 8:24], func=AF.Exp),
           reads=("SMALL_ES",), writes=("SMALL_ES",))
    for l in range(depth):
        for j in range(3):
            load_T(conv_w[l, j], FC, CONVW[:, (l * 3 + j) * FC:(l * 3 + j + 1) * FC], ("CONVW", l, j))
        load_T(conv_b[l], FC, CONVB[:, l * FC:(l + 1) * FC], ("CONVB", l))

    for s in range(NSEQ):
        load_T(c_in[s], KC, CT[:, s * KC:(s + 1) * KC], ("CT", s))
    sch.op("act", lambda e: e.activation(out=SCT[:, :], in_=CT[:, :], func=AF.Sigmoid),
           reads=[("CT", s) for s in range(NSEQ)], writes=("SCT",))
    sch.op("dve", lambda e: e.tensor_tensor(SCT[:, :], SCT[:, :], CT[:, :], ALU.mult),
           reads=[("CT", s) for s in range(NSEQ)] + ["SCT"], writes=("SCT",))
    SCTv = SCT[:, :].rearrange("p (s k) -> p k s", s=NSEQ)
    WBF = [WB[0][:, :].bitcast(F32), WB[1][:, :].bitcast(F32)]
    ADA_NC = 768

    def modp(s, l, which):
        o = ((s * DEPTH + l) * 6 + which) * KC
        return MODP[:, o:o + KC]

    slab = 0
    for l in range(depth):
        load_T(b_ada[l], 96, BADA[:, :], "BADA")
        load_T(norm_attn[l], KC, NAT[:, 0:KC], "NAT0")
        load_T(norm_ffn[l], KC, NAT[:, KC:2 * KC], "NAT1")
        ps, pk = psb()
        for sl in range(6 * D // ADA_NC):
            h = slab % 2
            slab += 1
            wv = WBF[h][:, 0:KC * ADA_NC].rearrange("p (k n) -> p k n", k=KC)
            for g in range(4):
                sch.dma("sp", wv[:, g * 4:(g + 1) * 4, :],
                        w_ada[l, g * 512:(g + 1) * 512, sl * ADA_NC:(sl + 1) * ADA_NC].rearrange("(k p) n -> p k n", p=128),
                        writes=(("WB", h, g),))
            for j in range(ADA_NC // 128):
                oc = sl * (ADA_NC // 128) + j
                for kc in range(KC):
                    mm(ps[:, oc * NSEQ:(oc + 1) * NSEQ], wv[:, kc, j * 128:(j + 1) * 128], SCTv[:, kc, :],
                       kc == 0, kc == KC - 1, reads=(("WB", h, kc // 4), "SCT"), writes=(pk,))
        for s in range(NSEQ):
            sch.op("dve", lambda e, s=s, ps=ps: e.tensor_tensor(
                MODRAW[:, s * 96:(s + 1) * 96], ps[:, 0:96 * NSEQ].rearrange("p (o s) -> p s o", s=NSEQ)[:, s, :],
                BADA[:, :], ALU.add), reads=(pk, "BADA"), writes=(("MODRAW", s),))
            R = lambda w, s=s: MODRAW[:, s * 96 + w * KC: s * 96 + (w + 1) * KC]
            for (dst, a_src, nat) in ((0, 1, 0), (3, 4, 1)):
                sch.op("dve", lambda e, dst=dst, a_src=a_src, nat=nat, s=s, l=l, R=R: e.scalar_tensor_tensor(
                    modp(s, l, dst), R(a_src), 1.0, NAT[:, nat * KC:(nat + 1) * KC], ALU.add, ALU.mult),
                    reads=(("MODRAW", s), "NAT0", "NAT1"), writes=(("MODP", s, l, dst),))
            for (dst, src) in ((1, 0), (2, 2), (4, 3), (5, 5)):
                sch.op("dve", lambda e, dst=dst, src=src, s=s, l=l, R=R: e.tensor_copy(modp(s, l, dst), R(src)),
                       reads=(("MODRAW", s),), writes=(("MODP", s, l, dst),))

    sch.op("dve", lambda e: e.memset(SMALL[:, 40:41], 0.0),
           reads=[("MODP", s_, l_, w_) for s_ in range(NSEQ) for l_ in range(depth) for w_ in range(6)], writes=("MODPALL",))

    def wb_view(h, kcn, ncols):
        return WB[h][:, 0:kcn * ncols].rearrange("p (k n) -> p k n", k=kcn)

    def load_w(h, src2d, k0, kcn, col_ranges):
        ncols = sum(c1 - c0 for c0, c1 in col_ranges)
        wv = wb_view(h, kcn, ncols)
        step = 4
        for g0 in range(0, kcn, step):
            g1 = min(kcn, g0 + step)
            o = 0
            for c0, c1 in col_ranges:
                sch.dma("pool", wv[:, g0:g1, o:o + (c1 - c0)],
                        src2d[(k0 + g0) * 128:(k0 + g1) * 128, c0:c1].rearrange("(k p) n -> p k n", p=128),
                        writes=(("WB", h, g0 // step), ("WB", h, "att", 0), ("WB", h, "att", 1)))
                o += c1 - c0
        return wv

    def wkeys(h, kc, col_ranges):
        return [("WB", h, kc // 4)]

    wb_next = [0]

    def next_wb():
        h = wb_next[0]
        wb_next[0] ^= 1
        return h

    def rstd_from_ps(ps, pk, T, n_feat, dst=None):
        r, rk = dst if dst is not None else sf()
        sch.op("act", lambda e: e.activation(out=r[:, 0:T], in_=ps[:, 0:T], func=AF.Sqrt, bias=EPS, scale=1.0 / n_feat),
               reads=(pk,), writes=(rk,))
        sch.op("dve", lambda e: e.reciprocal(r[:, 0:T], r[:, 0:T]), reads=(rk,), writes=(rk,))
        return r, rk

    def phase_transpose_in(s):
        S = S_list[s]
        T = 256
        RBv = RB[:, :].rearrange("p (u f) -> p u f", u=2)[:, :, 0:4096].rearrange("p u (b f) -> p u b f", b=2)
        nt = S // T
        us = [next_slot("RB") for _ in range(nt)]

        def load(ti):
            for b in range(2):
                sch.dma("sp", RBv[:, us[ti], b, :], x_in[s][ti * T + b * 128:ti * T + (b + 1) * 128, :], writes=(("RB", us[ti]),))

        load(0)
        for ti in range(nt):
            u = us[ti]
            t0 = ti * T
            if ti + 1 < nt:
                load(ti + 1)
            for kc in range(KC):
                ps, pk = psb()
                for b in range(2):
                    sch.op("pe", lambda e, ps=ps, b=b, u=u, kc=kc: e.transpose(
                        ps[:, b * 128:(b + 1) * 128], RBv[:, u, b, kc * 128:(kc + 1) * 128], IDENT[:, :]),
                        reads=(("RB", u), "IDENT"), writes=(pk,), signal=(b == 1))
                st, sk = sf()
                copy_op(evac_eng(), st[:, 0:T], ps[:, 0:T], (pk,), (sk,))
                sch.dma(STQ, XT[s][kc, :, t0:t0 + T], st[:, 0:T], reads=(sk,), writes=(("XT", s, kc),))

    def phase_norm(s, A_ap, B_ap, final=False):
        S = S_list[s]
        T = 256
        RBv = RB[:, :].rearrange("p (u f) -> p u f", u=2)[:, :, 0:4096].rearrange("p u (k t) -> p u k t", k=KC)
        nt = S // T
        us = [next_slot("RB") for _ in range(nt)]
        uas = [next_slot("RA") for _ in range(nt)]

        def load(ti):
            sch.dma("sp", RBv[:, us[ti], :, :], XT[s][:, :, ti * T:(ti + 1) * T].rearrange("k p t -> p k t"),
                    reads=[("XT", s, kc) for kc in range(KC)], writes=(("RB", us[ti]),))

        load(0)
        for ti in range(nt):
            u = us[ti]
            ua = uas[ti]
            t0 = ti * T
            if ti + 1 < nt:
                load(ti + 1)
            ps, pk = psb()
            for kc in range(KC):
                q, qk = sh()
                sch.op("act", lambda e, q=q, u=u, kc=kc: e.activation(out=q[:, 0:T], in_=RBv[:, u, kc, :], func=AF.Square),
                       reads=(("RB", u),), writes=(qk,))
                mm(ps[:, 0:T], ONESB[:, :], q[:, 0:T], kc == 0, kc == KC - 1, reads=(qk, "ONESB"), writes=(pk,), force=True)
            r, rk = rstd_from_ps(ps, pk, T, D, dst=(RSTD[u], ("RSTD", u)))
            if not final:
                hout = RA[ua][:, 0:KC * T].rearrange("p (k t) -> p k t", k=KC)
                for kc in range(KC):
                    tm, tk = sf()
                    sch.op("dve", lambda e, tm=tm, u=u, kc=kc, r=r: e.tensor_tensor(tm[:, 0:T], RBv[:, u, kc, :], r[:, 0:T], ALU.mult),
                           reads=(("RB", u), rk), writes=(tk,))
                    sch.op("act", lambda e, tm=tm, kc=kc, hout=hout: e.activation(
                        out=hout[:, kc, :], in_=tm[:, 0:T], func=AF.Identity, bias=B_ap[:, kc:kc + 1], scale=A_ap[:, kc:kc + 1]),
                        reads=(tk, "MODPALL"), writes=(("RA", ua),))
                sch.dma(STQ, HT[s][:, :, t0:t0 + T].rearrange("k p t -> p k t"), hout[:, :, :],
                        reads=(("RA", ua),), writes=(("HT", s),))
            else:
                for b in range(T // 128):
                    orow = RA[ua][:, b * 4096:(b + 1) * 4096].bitcast(F32)
                    okey = ("RAo", ua, b)
                    for kc in range(KC):
                        tm, tk = sf()
                        sch.op("dve", lambda e, tm=tm, u=u, kc=kc, r=r, b=b: e.scalar_tensor_tensor(
                            tm[:, 0:128], RBv[:, u, kc, b * 128:(b + 1) * 128], A_ap[:, kc:kc + 1], r[:, b * 128:(b + 1) * 128],
                            ALU.mult, ALU.mult), reads=(("RB", u), rk, "NFIN"), writes=(tk,))
                        if kc % 4 == 0:
                            ps2, pk2 = psb()
                        sch.op("pe", lambda e, ps2=ps2, tm=tm, kc=kc: e.transpose(
                            ps2[:, (kc % 4) * 128:(kc % 4 + 1) * 128], tm[:, 0:128], IDENT[:, :]),
                            reads=(tk, "IDENT"), writes=(pk2,), signal=True)
                        if kc % 4 == 3:
                            copy_op(evac_eng(), orow[:, (kc - 3) * 128:(kc + 1) * 128], ps2[:, :], (pk2,), (okey, ("RA", ua)))
                    sch.dma(STQ, y_out[s][t0 + b * 128:t0 + (b + 1) * 128, :], orow[:, :],
                            reads=(okey,), writes=(("Y", s),))

    def linear_pass(s, src, src_key, kcn, wv, h, col_ranges, n_oc, epilogue, T, tiles=None, in_buf="RA", v_token_major=None,
                    pre_tile=None, oc_order=None):
        S = S_list[s]
        if tiles is None:
            tiles = [(t0, T, t0, T, 0) for t0 in range(0, S, T)]
        if oc_order is None:
            oc_order = list(range(n_oc))

        us = [next_slot(in_buf) for _ in tiles]

        def bufof(ti):
            u = us[ti]
            if in_buf == "RA":
                return RA[u][:, 0:kcn * T].rearrange("p (k t) -> p k t", k=kcn), ("RA", u)
            return RB[:, :].bitcast(BF16)[:, u * 11264:u * 11264 + kcn * T].rearrange("p (k t) -> p k t", k=kcn), ("RB", u)

        def load(ti):
            lo, n, olo, on, coff = tiles[ti]
            buf, bkey = bufof(ti)
            vlo, vhi = max(lo, 0), min(lo + n, S)
            if vlo > lo or vhi < lo + n:
                sch.op("dve", lambda e, buf=buf, n=n: e.memset(buf[:, :, 0:n], 0.0), writes=(bkey,))
            sch.dma("sp", buf[:, :, vlo - lo:vhi - lo], src[:, :, vlo:vhi].rearrange("k p t -> p k t"),
                    reads=(src_key,), writes=(bkey,))
            if pre_tile is not None:
                pre_tile(ti, tiles[ti])

        load(0)
        for ti, (lo, n, olo, on, coff) in enumerate(tiles):
            if ti + 1 < len(tiles):
                load(ti + 1)
            buf, bkey = bufof(ti)
            rd_in = (bkey,)
            for oc in oc_order:
                ps, pk = psb()
                for kc in range(kcn):
                    mm(ps[:, 0:n], wv[:, kc, oc * 128:(oc + 1) * 128], buf[:, kc, 0:n], kc == 0, kc == kcn - 1,
                       reads=tuple(wkeys(h, kc, col_ranges)) + rd_in, writes=(pk,))
                epilogue(oc, ps, pk, (lo, n, olo, on, coff), ti)
            if v_token_major is not None:
                vc0, vn, vdst0 = v_token_major
                for tb in range(n // 128):
                    for c in range(0, vn, 512):
                        cw = min(512, vn - c)
                        ps, pk = psb()
                        for kc in range(kcn):
                            mm(ps[:, 0:cw], buf[:, kc, tb * 128:(tb + 1) * 128], wv[:, kc, vc0 + c:vc0 + c + cw], kc == 0, kc == kcn - 1,
                               reads=tuple(wkeys(h, kc, col_ranges)) + rd_in, writes=(pk,))
                        st, sk = sh()
                        copy_op(evac_eng(), st[:, 0:cw], ps[:, 0:cw], (pk,), (sk,))
                        sch.dma(STQ, VV[s][lo + tb * 128:lo + (tb + 1) * 128, vdst0 + c:vdst0 + c + cw], st[:, 0:cw],
                                reads=(sk,), writes=(("VV", s),))

    def phase_qkv(s, l):
        m = l % 4
        S = S_list[s]
        T = 512
        w = wqkv[m]
        if m == 2:
            nq, nk, nv = 16, 8, 1024
        else:
            nq, nk, nv = 16, 4, 512
        fm_chunks = [("q", i) for i in range(nq)] + [("k", i) for i in range(nk)]
        groups = [fm_chunks[i:i + 12] for i in range(0, len(fm_chunks), 12)]
        vcol0 = (nq + nk) * 128
        plans = []
        for gi, g in enumerate(groups):
            cols = [(((0 if kind == "q" else nq) + i) * 128) for kind, i in g]
            cr = [(cols[0], cols[-1] + 128)]
            vt = None
            if gi == len(groups) - 1 and (len(g) * 128 + nv) * KC <= 24576:
                cr = [(cols[0], cols[-1] + 128 + nv)]
                vt = (len(g) * 128, nv, 0)
            plans.append((g, cr, vt))
        if plans[-1][2] is None:
            plans.append(([], [(vcol0, vcol0 + nv)], (0, nv, 0)))

        def make_epi(g):
            def epi(oc, ps, pk, tinfo, ti):
                lo, n, olo, on, coff = tinfo
                kind, i = g[oc]
                ROPEC, ROPES, rkey = rope_bufs(ti)
                dst = QT[s][i] if kind == "q" else KT[s][i]
                dkey = ("QT", s) if kind == "q" else ("KT", s)
                if m != 0:
                    st, sk = sh()
                    copy_op(evac_eng(), st[:, 0:n], ps[:, 0:n], (pk,), (sk,))
                    sch.dma(STQ, dst[:, lo:lo + n], st[:, 0:n], reads=(sk,), writes=(dkey,))
                    return
                gcol = SMALL[:, 0:1] if kind == "q" else SMALL[:, 1:2]
                sq, sqk = sh()
                sch.op("act", lambda e: e.activation(out=sq[:, 0:n], in_=ps[:, 0:n], func=AF.Square), reads=(pk,), writes=(sqk,))
                ps2, pk2 = psb()
                mm(ps2[:, 0:n], ONESB[:, :], sq[:, 0:n], True, True, reads=(sqk, "ONESB"), writes=(pk2,))
                r, rk = rstd_from_ps(ps2, pk2, n, HD)
                qn, qnk = sf()
                sch.op("dve", lambda e: e.scalar_tensor_tensor(qn[:, 0:n], ps[:, 0:n], gcol, r[:, 0:n], ALU.mult, ALU.mult),
                       reads=(pk, rk, "SMALL_QN", "SMALL_KN"), writes=(qnk,))
                qb, qbk = sh()
                sch.op("act", lambda e: e.activation(out=qb[:, 0:n], in_=qn[:, 0:n], func=AF.Copy), reads=(qnk,), writes=(qbk,))
                ps3, pk3 = psb()
                mm(ps3[:, 0:n], PMB[:, :], qb[:, 0:n], True, True, reads=(qbk, "PMB"), writes=(pk3,))
                t1, t1k = sf()
                sch.op("dve", lambda e: e.tensor_tensor(t1[:, 0:n], qn[:, 0:n], ROPEC[:, 0:n], ALU.mult),
                       reads=(qnk, rkey), writes=(t1k,))
                t2, t2k = sf()
                sch.op("dve", lambda e: e.tensor_tensor(t2[:, 0:n], ps3[:, 0:n], ROPES[:, 0:n], ALU.mult),
                       reads=(pk3, rkey), writes=(t2k,))
                st, sk = sh()
                sch.op("dve", lambda e: e.tensor_tensor(st[:, 0:n], t1[:, 0:n], t2[:, 0:n], ALU.add),
                       reads=(t1k, t2k), writes=(sk,))
                sch.dma(STQ, dst[:, lo:lo + n], st[:, 0:n], reads=(sk,), writes=(dkey,))
            return epi

        wvs = [None] * len(plans)
        pf = hoisted.pop(("qkv", s, l), None)
        if pf is not None and plans[0][1] == [(0, 1536)]:
            hs = [pf[0]] + [next_wb() for _ in plans[1:]]
            wvs[0] = pf[1]
        else:
            hs = [next_wb() for _ in plans]
            wvs[0] = load_w(hs[0], w, 0, KC, plans[0][1])
        for pi, (g, cr, vt) in enumerate(plans):
            if pi + 1 < len(plans):
                wvs[pi + 1] = load_w(hs[pi + 1], w, 0, KC, plans[pi + 1][1])
            pre = None
            if m == 0 and len(g):
                def pre(ti, tinfo):
                    c_, s_, rkey = rope_bufs(ti)
                    sch.dma("sp", c_[:, 0:T], rope_c_d[:, tinfo[0]:tinfo[0] + T], writes=(rkey,))
                    sch.dma("sp", s_[:, 0:T], rope_s_d[:, tinfo[0]:tinfo[0] + T], writes=(rkey,))
            linear_pass(s, HT[s], ("HT", s), KC, wvs[pi], hs[pi], cr, len(g), make_epi(g), T, v_token_major=vt, pre_tile=pre)

    def rope_bufs(ti):
        u = ti % 2
        return RB[:, u * 5632:u * 5632 + 512], RB[:, u * 5632 + 512:u * 5632 + 1024], ("RB", u)

    def phase_attn(s, l):
        m = l % 4
        S = S_list[s]
        NQ = 512
        TABSINGLE = os.environ.get("K_TABSINGLE", "0") == "1"
        LA = int(os.environ.get("K_LA", "2"))
        nblk = S // 128
        M = 2 if m == 2 else 1
        DV = 2 if m == 2 else 1
        G = 2 if m == 2 else 4
        table = {0: None, 1: (tb_d, TB_CEN, TB_L), 2: (tc_d, TC_CEN, TC_L), 3: (td_d, TD_CEN, TD_L)}[m]
        KB16 = WB[0]
        VB16 = WB[1]
        na = DV + 1
        if m == 2:
            acc_sets = [0]
            rot_lo = na
        else:
            acc_sets = [0, na]
            rot_lo = 2 * na
        allw0 = [("WB", 0, i) for i in range(11)]
        allw1 = [("WB", 1, i) for i in range(11)]
        U0 = RA[0][:, 0:2048].bitcast(F32)

        def kv_views(kvh):
            par = kvh % 2
            kview = KB16[:, par * M * S:(par + 1) * M * S].rearrange("p (m t) -> p m t", m=M)
            vview = VB16[:, par * nblk * DV * 128:(par + 1) * nblk * DV * 128].rearrange("p (b d) -> p b d", b=nblk)
            return kview, vview, ("WB", 0, "att", par), ("WB", 1, "att", par)

        def load_kv(kvh):
            kview, vview, kkey, vkey = kv_views(kvh)
            for mm_ in range(M):
                sch.dma("sp", kview[:, mm_, :], KT[s][kvh * M + mm_], reads=(("KT", s),), writes=tuple([kkey] + allw0))
            for b0 in range(0, nblk, 8):
                b1 = min(nblk, b0 + 8)
                sch.dma("sp", vview[:, b0:b1, :],
                        VV[s][b0 * 128:b1 * 128, kvh * DV * 128:(kvh + 1) * DV * 128].rearrange("(b p) d -> p b d", p=128),
                        reads=(("VV", s),), writes=tuple([vkey] + allw1))

        items = [(kvh, g, qi, mm_) for kvh in range(4) for g in range(G) for qi in range(S // NQ) for mm_ in range(M)]

        def qslot(ii):
            return ii % 2

        def load_q(ii):
            kvh, g, qi, mm_ = items[ii]
            h = kvh * G + g
            sl = qslot(ii)
            sch.dma("sp", QTILE[:, sl * 512:(sl + 1) * 512], QT[s][h * M + mm_][:, qi * NQ:(qi + 1) * NQ],
                    reads=(("QT", s),), writes=(("QTILE", sl),))

        def tab_view(h):
            tab, cen, L = table
            if m == 2 or TABSINGLE:
                return RB[:, 0:L], (("RB", 0), ("RB", 1))
            u = h % 2
            return RB[:, u * 5632:u * 5632 + L], (("RB", u),)

        def load_tab(h):
            tab, cen, L = table
            tv, tk = tab_view(h)
            sch.dma("sp", tv, tab[h], writes=tk)

        load_kv(0)
        if table is not None:
            load_tab(0)
        load_q(0)
        st1, st2 = [], []
        pres = {}

        def make_pass(ii):
            kvh, g, qi, mm_ = items[ii]
            h = kvh * G + g
            kview, vview, kkey, vkey = kv_views(kvh)
            first_of_kvh = (g == 0 and qi == 0 and mm_ == 0)
            first_of_head = (qi == 0 and mm_ == 0)
            q0 = qi * NQ
            qb0 = q0 // 128
            if m == 1:
                kbs = [kb for kb in range(qb0 - 1, qb0 + 5) if 0 <= kb < nblk]
            elif m == 3:
                kbs = [kb for kb in range(qb0 - 8, qb0 + 12) if 0 <= kb < nblk]
            else:
                kbs = list(range(nblk))
            sl = qslot(ii)
            qt_, qk_ = QTILE[:, sl * 512:(sl + 1) * 512], ("QTILE", sl)
            base = acc_sets[ii % len(acc_sets)]
            accO = [(PS[base + dv], ("PS", base + dv)) for dv in range(DV)]
            accZ = (PS[base + DV], ("PS", base + DV))
            tv, tkeys, cen = None, (), 0
            if table is not None:
                tv, tkeys = tab_view(h)
                cen = table[1]
            pt = {}
            nk = len(kbs)

            def pre():
                if first_of_kvh and kvh + 1 < 4:
                    load_kv(kvh + 1)
                if first_of_head and table is not None and h + 1 < 4 * G and m != 2 and not TABSINGLE:
                    load_tab(h + 1)
                if ii + 1 < len(items):
                    load_q(ii + 1)

            pres[ii] = pre

            def stage1(idx):
                if idx == 0 and first_of_head and table is not None and (m == 2 or TABSINGLE) and h > 0:
                    load_tab(h)
                kb = kbs[idx]
                ps, pk = psb(rot_lo, 8)
                mm(ps[:, 0:NQ], kview[:, mm_, kb * 128:(kb + 1) * 128], qt_[:, 0:NQ], True, True, reads=(kkey, qk_), writes=(pk,))
                p_, pkk = sh()
                if table is None:
                    sch.op("act", lambda e, p_=p_, ps=ps: e.activation(out=p_[:, 0:NQ], in_=ps[:, 0:NQ], func=AF.Exp, scale=SCALE),
                           reads=(pk,), writes=(pkk,))
                else:
                    e_, ek = sf()
                    sch.op("act", lambda e, e_=e_, ps=ps: e.activation(out=e_[:, 0:NQ], in_=ps[:, 0:NQ], func=AF.Exp, scale=SCALE),
                           reads=(pk,), writes=(ek,))
                    c0 = cen - (kb * 128 - q0)
                    sch.op("dve", lambda e, p_=p_, e_=e_, c0=c0, tv=tv: e.tensor_tensor(p_[:, 0:NQ], e_[:, 0:NQ], tv[:, c0:c0 + NQ], ALU.mult),
                           reads=(ek,) + tkeys, writes=(pkk,))
                pt[idx] = (p_, pkk)

            def stage2(idx):
                kb = kbs[idx]
                p_, pkk = pt.pop(idx)
                first, last = idx == 0, idx == nk - 1
                for dv in range(DV):
                    mm(accO[dv][0][:, 0:NQ], vview[:, kb, dv * 128:(dv + 1) * 128], p_[:, 0:NQ], first, last,
                       reads=(vkey, pkk), writes=(accO[dv][1],))
                mm(accZ[0][:, 0:NQ], ONESB[:, :], p_[:, 0:NQ], first, last, reads=("ONESB", pkk), writes=(accZ[1],), force=True)
                if last:
                    if ii + 1 in pres:
                        pres[ii + 1]()
                    epilogue()

            def epilogue():
                z, zk = sf()
                zp, zpk = accZ
                if m == 1:
                    sch.op("dve", lambda e, z=z, zp=zp, h=h: e.tensor_scalar(z[:, 0:NQ], zp[:, 0:NQ], SMALL[:, 8 + h:9 + h], None, op0=ALU.add),
                           reads=(zpk, "SMALL_ES"), writes=(zk,))
                    sch.op("dve", lambda e, z=z: e.reciprocal(z[:, 0:NQ], z[:, 0:NQ]), reads=(zk,), writes=(zk,))
                else:
                    sch.op("dve", lambda e, z=z, zp=zp: e.reciprocal(z[:, 0:NQ], zp[:, 0:NQ]), reads=(zpk,), writes=(zk,))
                if m != 2:
                    o_, ok = sh()
                    op_, opk = accO[0]
                    sch.op("dve", lambda e, o_=o_, op_=op_, z=z: e.tensor_tensor(o_[:, 0:NQ], op_[:, 0:NQ], z[:, 0:NQ], ALU.mult),
                           reads=(opk, zk), writes=(ok,))
                    sch.dma(STQ, OT[s][h][:, q0:q0 + NQ], o_[:, 0:NQ], reads=(ok,), writes=(("OT", s),))
                elif mm_ == 0:
                    for dv in range(DV):
                        sch.op("dve", lambda e, dv=dv, a=accO[dv][0], z=z: e.tensor_tensor(U0[:, dv * 512:dv * 512 + NQ], a[:, 0:NQ], z[:, 0:NQ], ALU.mult),
                               reads=(accO[dv][1], zk), writes=(("RA", 0),))
                else:
                    psn, pnk = psb(rot_lo, 8)
                    for dv in range(DV):
                        v_, vk = sf()
                        sch.op("dve", lambda e, v_=v_, a=accO[dv][0], z=z: e.tensor_tensor(v_[:, 0:NQ], a[:, 0:NQ], z[:, 0:NQ], ALU.mult),
                               reads=(accO[dv][1], zk), writes=(vk,))
                        sch.op("dve", lambda e, v_=v_, dv=dv: e.scalar_tensor_tensor(
                            U0[:, dv * 512:dv * 512 + NQ], v_[:, 0:NQ], SMALL[:, 4:5], U0[:, dv * 512:dv * 512 + NQ], ALU.mult, ALU.add),
                            reads=(vk, "SMALL_NL", ("RA", 0)), writes=(("RA", 0),))
                        q_, qk2 = sh()
                        sch.op("act", lambda e, q_=q_, dv=dv: e.activation(out=q_[:, 0:NQ], in_=U0[:, dv * 512:dv * 512 + NQ], func=AF.Square),
                               reads=(("RA", 0),), writes=(qk2,))
                        mm(psn[:, 0:NQ], ONESB[:, :], q_[:, 0:NQ], dv == 0, dv == DV - 1, reads=(qk2, "ONESB"), writes=(pnk,), force=True)
                    r, rk = rstd_from_ps(psn, pnk, NQ, 2 * HD)
                    for dv in range(DV):
                        o_, ok = sh()
                        sch.op("dve", lambda e, o_=o_, dv=dv, r=r: e.scalar_tensor_tensor(
                            o_[:, 0:NQ], U0[:, dv * 512:dv * 512 + NQ], SMALL[:, 2 + dv:3 + dv], r[:, 0:NQ], ALU.mult, ALU.mult),
                            reads=(("RA", 0), rk, "SMALL_SUBW"), writes=(ok,))
                        sch.dma(STQ, OT[s][h * 2 + dv][:, q0:q0 + NQ], o_[:, 0:NQ], reads=(ok,), writes=(("OT", s),))

            for idx in range(nk):
                st1.append((stage1, idx))
                st2.append((stage2, idx))

        for ii in range(len(items)):
            make_pass(ii)
        pres[0]()
        if m != 2:
            N = len(st1)
            for j in range(N + LA):
                if j < N:
                    st1[j][0](st1[j][1])
                if j >= LA:
                    st2[j - LA][0](st2[j - LA][1])
        else:
            j0 = 0
            while j0 < len(st1):
                f = st1[j0][0]
                j1 = j0
                while j1 < len(st1) and st1[j1][0] is f:
                    j1 += 1
                n = j1 - j0
                LAc = 3
                for j in range(n + LAc):
                    if j < n:
                        st1[j0 + j][0](st1[j0 + j][1])
                    if j >= LAc:
                        st2[j0 + j - LAc][0](st2[j0 + j - LAc][1])
                j0 = j1

    def phase_resid(s, src, src_key, kcn, w2d, gate_ap, T, oc_per, in_buf):
        S = S_list[s]
        n_sub = KC // oc_per
        hs = [next_wb() for _ in range(n_sub)]
        wvs = [None] * n_sub
        wvs[0] = load_w(hs[0], w2d, 0, kcn, [(0, oc_per * 128)])
        for si in range(n_sub):
            if si + 1 < n_sub:
                wvs[si + 1] = load_w(hs[si + 1], w2d, 0, kcn, [((si + 1) * oc_per * 128, (si + 2) * oc_per * 128)])
            cr = [(si * oc_per * 128, (si + 1) * oc_per * 128)]

            def epi(oc, ps, pk, tinfo, ti, si=si):
                lo, n, olo, on, coff = tinfo
                ocg = si * oc_per + oc
                xc, xk = sf()
                sch.dma("sp", xc[:, 0:n], XT[s][ocg, :, lo:lo + n], reads=(("XT", s, ocg),), writes=(xk,))
                sch.op("dve", lambda e: e.scalar_tensor_tensor(xc[:, 0:n], ps[:, 0:n], gate_ap[:, ocg:ocg + 1], xc[:, 0:n], ALU.mult, ALU.add),
                       reads=(pk, xk, "MODPALL"), writes=(xk,))
                sch.dma(STQ, XT[s][ocg, :, lo:lo + n], xc[:, 0:n], reads=(xk,), writes=(("XT", s, ocg),))

            linear_pass(s, src, src_key, kcn, wvs[si], hs[si], cr, oc_per, epi, T, in_buf=in_buf)

    def phase_ffn_up(s, l):
        S = S_list[s]
        T = 512
        step = T - 2
        tiles = []
        t0 = 0
        while t0 < S:
            on = min(step, S - t0)
            tiles.append((t0 - 1, on + 2, t0, on, 1))
            t0 += step
        subs = [list(range(i, min(i + 6, FC))) for i in range(0, FC, 6)]
        wvs = [None] * len(subs)

        def crs(fcs):
            return [(fcs[0] * 128, (fcs[-1] + 1) * 128), (DFF + fcs[0] * 128, DFF + (fcs[-1] + 1) * 128)]

        pf = hoisted.pop(("ffn", s, l), None)
        if pf is not None:
            hs = [pf[0]] + [next_wb() for _ in subs[1:]]
            wvs[0] = pf[1]
        else:
            hs = [next_wb() for _ in subs]
            wvs[0] = load_w(hs[0], w_up[l], 0, KC, crs(subs[0]))
        for si, fcs in enumerate(subs):
            if si + 1 < len(subs):
                wvs[si + 1] = load_w(hs[si + 1], w_up[l], 0, KC, crs(subs[si + 1]))
            nf = len(fcs)
            pend = {}

            def epi(oc, ps, pk, tinfo, ti, fcs=fcs, nf=nf, pend=pend):
                lo, n, olo, on, coff = tinfo
                if oc < nf:
                    fc = fcs[oc]
                    w0 = CONVW[:, (l * 3 + 0) * FC + fc:(l * 3 + 0) * FC + fc + 1]
                    w1 = CONVW[:, (l * 3 + 1) * FC + fc:(l * 3 + 1) * FC + fc + 1]
                    w2 = CONVW[:, (l * 3 + 2) * FC + fc:(l * 3 + 2) * FC + fc + 1]
                    cb = CONVB[:, l * FC + fc:l * FC + fc + 1]
                    u_, uk = sf()
                    ck = (("CONVW", l, 0), ("CONVW", l, 1), ("CONVW", l, 2), ("CONVB", l))
                    sch.op("dve", lambda e: e.tensor_scalar(u_[:, 0:on], ps[:, 0:on], w0, None, op0=ALU.mult), reads=(pk,) + ck, writes=(uk,))
                    sch.op("dve", lambda e: e.scalar_tensor_tensor(u_[:, 0:on], ps[:, 1:1 + on], w1, u_[:, 0:on], ALU.mult, ALU.add),
                           reads=(pk, uk) + ck, writes=(uk,))
                    sch.op("dve", lambda e: e.scalar_tensor_tensor(u_[:, 0:on], ps[:, 2:2 + on], w2, u_[:, 0:on], ALU.mult, ALU.add),
                           reads=(pk, uk) + ck, writes=(uk,))
                    sch.op("act", lambda e: e.activation(out=u_[:, 0:on], in_=u_[:, 0:on], func=AF.Gelu, bias=cb), reads=(uk,) + ck, writes=(uk,))
                    pend[oc] = (u_, uk)
                else:
                    fc = fcs[oc - nf]
                    u_, uk = pend.pop(oc - nf)
                    g_, gk = sh()
                    sch.op("dve", lambda e: e.tensor_tensor(g_[:, 0:on], u_[:, 0:on], ps[:, 1:1 + on], ALU.mult), reads=(uk, pk), writes=(gk,))
                    sch.dma(STQ, GT[s][fc][:, olo:olo + on], g_[:, 0:on], reads=(gk,), writes=(("GT", s),))

            order = [o for f in range(nf) for o in (f, nf + f)]
            linear_pass(s, HT[s], ("HT", s), KC, wvs[si], hs[si], crs(fcs), 2 * nf, epi, T, tiles=tiles, oc_order=order)

    def linear_pass_ab(s, wv, h, col_ranges, nf, epi, T, tiles):
        S = S_list[s]
        for ti, (lo, n, olo, on, coff) in enumerate(tiles):
            u = ti % 2
            buf = RA[u][:, 0:KC * T].rearrange("p (k t) -> p k t", k=KC)
            bkey = ("RA", u)
            vlo, vhi = max(lo, 0), min(lo + n, S)
            if vlo > lo or vhi < lo + n:
                sch.op("dve", lambda e, buf=buf, n=n: e.memset(buf[:, :, 0:n], 0.0), writes=(bkey,))
            sch.dma("sp", buf[:, :, vlo - lo:vhi - lo], HT[s][:, :, vlo:vhi].rearrange("k p t -> p k t"),
                    reads=(("HT", s),), writes=(bkey,))
            for f in range(nf):
                for oc in (f, nf + f):
                    ps, pk = psb()
                    for kc in range(KC):
                        mm(ps[:, 0:n], wv[:, kc, oc * 128:(oc + 1) * 128], buf[:, kc, 0:n], kc == 0, kc == KC - 1,
                           reads=tuple(wkeys(h, kc, col_ranges)) + (bkey,), writes=(pk,))
                    epi(oc, ps, pk, (lo, n, olo, on, coff))

    hoisted = {}

    def hoist_qkv(s, l):
        h = next_wb()
        hoisted[("qkv", s, l)] = (h, load_w(h, wqkv[l % 4], 0, KC, [(0, 1536)]))

    def hoist_ffn(s, l):
        h = next_wb()
        hoisted[("ffn", s, l)] = (h, load_w(h, w_up[l], 0, KC, [(0, 768), (DFF, DFF + 768)]))

    for s in range(NSEQ):
        phase_transpose_in(s)
    for l in range(depth):
        for s in range(NSEQ):
            hoist_qkv(s, l)
            phase_norm(s, modp(s, l, 0), modp(s, l, 1))
            phase_qkv(s, l)
            phase_attn(s, l)
            phase_resid(s, OT[s], ("OT", s), KC, wo[l % 4], modp(s, l, 2), 512, 8, "RA")
            hoist_ffn(s, l)
            phase_norm(s, modp(s, l, 3), modp(s, l, 4))
            phase_ffn_up(s, l)
            phase_resid(s, GT[s], ("GT", s), FC, w_down[l], modp(s, l, 5), 256, 4, "RB")
    for s in range(NSEQ):
        phase_norm(s, NFIN, None, final=True)

    sch.finish()
    sch.emit()
    es.close()
    return nc


_PROG_CACHE = {}


def _get_prog(S_list, depth=DEPTH):
    key = (tuple(S_list), depth)
    if key not in _PROG_CACHE:
        _PROG_CACHE[key] = build_program(list(S_list), depth)
    return _PROG_CACHE[key]


def run_cores(xs_per_core, cs_per_core, weights, S_list, depth=DEPTH):
    nc = _get_prog(S_list, depth)
    consts = _const_tables()
    f = lambda a: np.ascontiguousarray(np.asarray(a, dtype=np.float32))
    shared = {
        "norm_attn": f(weights["norm_attn"]).reshape(DEPTH, KC, 128),
        "norm_ffn": f(weights["norm_ffn"]).reshape(DEPTH, KC, 128),
        "w_ada": f(weights["w_ada"]),
        "b_ada": f(weights["b_ada"]).reshape(DEPTH, 96, 128),
        "a_wqkv": f(weights["a_wqkv"])[0], "b_wqkv": f(weights["b_wqkv"])[0],
        "c_wqkv": f(weights["c_wqkv"])[0], "d_wqkv": f(weights["d_wqkv"])[0],
        "a_wo": f(weights["a_wo"])[0], "b_wo": f(weights["b_wo"])[0],
        "c_wo": f(weights["c_wo"])[0], "d_wo": f(weights["d_wo"])[0],
        "a_q_norm": f(weights["a_q_norm"]).reshape(1, 128),
        "a_k_norm": f(weights["a_k_norm"]).reshape(1, 128),
        "b_sink": f(weights["b_sink"]).reshape(1, 16),
        "c_lambda": f(weights["c_lambda"]).reshape(4, 128),
        "c_subln": f(weights["c_subln"]).reshape(2, 128),
        "ffn_w_up": f(weights["ffn_w_up"]),
        "ffn_conv_w": f(weights["ffn_conv_w"]).reshape(DEPTH, 3, FC, 128),
        "ffn_conv_b": f(weights["ffn_conv_b"]).reshape(DEPTH, FC, 128),
        "ffn_w_down": f(weights["ffn_w_down"]),
        "norm_final": f(weights["norm_final"]).reshape(KC, 128),
    }
    shared.update(consts)
    in_maps = []
    for c in range(len(xs_per_core)):
        m = dict(shared)
        for s in range(len(S_list)):
            m["x%d" % s] = f(xs_per_core[c][s])
            m["c%d" % s] = f(cs_per_core[c][s]).reshape(KC, 128)
        in_maps.append(m)
    res = run_bass_kernel_spmd(nc, in_maps, core_ids=list(range(len(in_maps))))
    return [[np.asarray(r["y%d" % s]) for s in range(len(S_list))] for r in res.results]


def kernel(x_prompt, x_sample, c_prompt, c_sample, **weights):
    x_prompt = np.asarray(x_prompt, dtype=np.float32)
    x_sample = np.asarray(x_sample, dtype=np.float32)
    c_prompt = np.asarray(c_prompt, dtype=np.float32)
    c_sample = np.asarray(c_sample, dtype=np.float32)
    nb_p = x_prompt.shape[0]
    xs, cs = [], []
    for c in range(N_CORES):
        xs.append([x_sample[c], x_prompt[c % nb_p]])
        cs.append([c_sample[c], c_prompt[c % nb_p]])
    outs = run_cores(xs, cs, weights, [x_sample.shape[1], x_prompt.shape[1]])
    y_sample = np.stack([outs[c][0] for c in range(N_CORES)], axis=0)
    y_prompt = np.stack([outs[c][1] for c in range(nb_p)], axis=0)
    return (y_prompt.astype(np.float32), y_sample.astype(np.float32))
```

```python
import math
import os
from contextlib import ExitStack

import numpy as np
import concourse.bass as bass
import concourse.mybir as mybir
from concourse.bass_utils import run_bass_kernel_spmd

F32 = mybir.dt.float32
BF16 = mybir.dt.bfloat16
AF = mybir.ActivationFunctionType
ALU = mybir.AluOpType

D = 2048
KC = 16
DFF = 5504
FC = 43
HD = 128
EPS = 1e-6
DEPTH = 4
SCALE = HD ** -0.5
N_CORES = 8


STQ = os.environ.get("K_STQ", "pool")
NODEDUP = os.environ.get("K_NODEDUP", "0") == "1"


class Sched:
    ENGS = ("pe", "act", "dve", "pool", "sp")

    def __init__(self, nc, es, n_dma_sems=56):
        self.nc = nc
        self.ops = {e: [] for e in self.ENGS}
        self.esem = {e: es.enter_context(nc.semaphore("s_" + e)) for e in ("pe", "act", "dve", "pool")}
        self.ecount = {e: 0 for e in self.ENGS}
        self.dsem = [es.enter_context(nc.semaphore("d%d" % i)) for i in range(n_dma_sems)]
        self.dcount = [0] * n_dma_sems
        self.dnext = 0
        self.dnext_pool = 0
        self.waited = {e: {} for e in self.ENGS}
        self.lastw = {}
        self.readers = {}

    def _deps(self, reads, writes):
        deps = {}

        def add(k, v):
            if deps.get(k, 0) < v:
                deps[k] = v

        for b in reads:
            t = self.lastw.get(b)
            if t:
                add(*t)
        for b in writes:
            t = self.lastw.get(b)
            if t:
                add(*t)
            for k, v in self.readers.get(b, {}).items():
                add(k, v)
        return deps

    def _reduce(self, eng, deps):
        out = []
        w = self.waited[eng]
        for k, v in deps.items():
            if k == ("e", "pe") and eng == "pe":
                continue
            if w.get(k, 0) < v or (NODEDUP and k[0] == "d" and eng in ("dve", "act")):
                w[k] = max(v, w.get(k, 0))
                out.append((k, v))
        return out

    def _commit(self, tok, reads, writes):
        for b in writes:
            self.lastw[b] = tok
            self.readers[b] = {}
        for b in reads:
            r = self.readers.setdefault(b, {})
            if r.get(tok[0], 0) < tok[1]:
                r[tok[0]] = tok[1]

    def op(self, eng, fn, reads=(), writes=(), signal=True):
        waits = self._reduce(eng, self._deps(reads, writes))
        if signal:
            self.ecount[eng] += 1
            tok = (("e", eng), self.ecount[eng])
        else:
            tok = (("e", eng), self.ecount[eng] + 1)
        self.ops[eng].append((waits, fn, True if signal else None))
        self._commit(tok, reads, writes)
        return tok

    def dma(self, q, out, in_, reads=(), writes=()):
        NP = 24
        if q == "pool":
            i = self.dnext_pool
            self.dnext_pool = (i + 1) % NP
        else:
            i = NP + self.dnext
            self.dnext = (self.dnext + 1) % (len(self.dsem) - NP)
        deps = self._deps(reads, writes)
        if self.dcount[i]:
            k = ("d", i)
            if deps.get(k, 0) < self.dcount[i]:
                deps[k] = self.dcount[i]
        waits = self._reduce(q, deps)
        self.dcount[i] += 16
        tok = (("d", i), self.dcount[i])
        self.ops[q].append((waits, (lambda e, o=out, s=in_: e.dma_start(out=o, in_=s)), ("d", i)))
        self._commit(tok, reads, writes)
        return tok

    def _sem(self, k):
        return self.esem[k[1]] if k[0] == "e" else self.dsem[k[1]]

    def finish(self):
        deps = {}
        for i, c in enumerate(self.dcount):
            if c:
                deps[("d", i)] = c
        for e in ("pe", "act", "dve", "pool"):
            if self.ecount[e]:
                deps[("e", e)] = self.ecount[e]
        waits = self._reduce("sp", deps)
        self.ops["sp"].append((waits, None, None))

    def _emit(self, name, e):
        for waits, fn, sig in self.ops[name]:
            for k, v in waits:
                e.wait_ge(self._sem(k), v)
            if fn is None:
                continue
            ins = fn(e)
            if sig is True:
                ins.then_inc(self.esem[name], 1)
            elif sig is not None:
                ins.then_inc(self.dsem[sig[1]], 16)

    def emit(self):
        with self.nc.Block() as block:
            @block.tensor
            def _(e):
                self._emit("pe", e)

            @block.scalar
            def _(e):
                self._emit("act", e)

            @block.vector
            def _(e):
                self._emit("dve", e)

            @block.gpsimd
            def _(e):
                self._emit("pool", e)

            @block.sync
            def _(e):
                self._emit("sp", e)


def _alibi_slopes(n):
    return 2.0 ** (-8.0 * np.arange(1, n + 1, dtype=np.float64) / n)


TB_CEN, TB_L = 512, 1152
TD_CEN, TD_L = 1408, 2944
TC_CEN, TC_L = 3968, 8064


def _toeplitz(fvals_fn, cen, L):
    i = np.arange(128)[:, None]
    c = np.arange(L)[None, :]
    d = i - c + cen
    return fvals_fn(d)


def _const_tables():
    sl16 = _alibi_slopes(16)
    sl8 = _alibi_slopes(8)
    tb = np.zeros((16, 128, TB_L), np.float32)
    td = np.zeros((16, 128, TD_L), np.float32)
    tc = np.zeros((8, 128, TC_L), np.float32)

    def mult_d(d):
        a = np.abs(d)
        m = (a <= 64).astype(np.float64)
        m += ((a <= 256) & (a % 4 == 0)).astype(np.float64)
        m += ((a <= 1024) & (a % 16 == 0)).astype(np.float64)
        return m

    for h in range(16):
        tb[h] = _toeplitz(lambda d: (np.abs(d) <= 128) * np.exp(-sl16[h] * np.abs(d)), TB_CEN, TB_L)
        td[h] = _toeplitz(lambda d: mult_d(d) * np.exp(-sl16[h] * np.abs(d)), TD_CEN, TD_L)
    for h in range(8):
        tc[h] = _toeplitz(lambda d: np.exp(-sl8[h] * np.abs(d)), TC_CEN, TC_L)
    t = np.arange(4096, dtype=np.float64)
    row, col = np.floor(t / 64), t % 64
    inv = 10000.0 ** (-np.arange(32, dtype=np.float64) / 32)
    rc = np.zeros((128, 4096), np.float32)
    rs = np.zeros((128, 4096), np.float32)
    pm = np.zeros((128, 128), np.float32)
    for dd in range(128):
        a, m, f = dd // 64, (dd // 32) % 2, dd % 32
        ang = (row if a == 0 else col) * inv[f]
        rc[dd] = np.cos(ang)
        rs[dd] = np.sin(ang) * (-1.0 if m == 0 else 1.0)
        src = a * 64 + (1 - m) * 32 + f
        pm[src, dd] = 1.0
    ident = np.eye(128, dtype=np.float32)
    return dict(tb=tb, td=td, tc=tc, rope_c=rc, rope_s=rs, pm=pm, ident=ident)


def _lambda_init(layer):
    return 0.8 - 0.6 * math.exp(-0.3 * layer)


def build_program(S_list, depth=DEPTH):
    nc = bass.Bass("TRN2", target_bir_lowering=False)
    es = ExitStack()
    NSEQ = len(S_list)

    def din(name, shape, dt=F32):
        return nc.dram_tensor(name, list(shape), dt, kind="ExternalInput").ap()

    def dscr(name, shape, dt):
        return nc.dram_tensor(name, list(shape), dt, kind="Internal").ap()

    x_in = [din("x%d" % s, [S, D]) for s, S in enumerate(S_list)]
    c_in = [din("c%d" % s, [KC, 128]) for s in range(NSEQ)]
    y_out = [nc.dram_tensor("y%d" % s, [S, D], F32, kind="ExternalOutput").ap() for s, S in enumerate(S_list)]
    norm_attn = din("norm_attn", [DEPTH, KC, 128])
    norm_ffn = din("norm_ffn", [DEPTH, KC, 128])
    w_ada = din("w_ada", [DEPTH, D, 6 * D])
    b_ada = din("b_ada", [DEPTH, 96, 128])
    wqkv = [din("a_wqkv", [D, 3072]), din("b_wqkv", [D, 3072]), din("c_wqkv", [D, 4096]), din("d_wqkv", [D, 3072])]
    wo = [din("a_wo", [D, D]), din("b_wo", [D, D]), din("c_wo", [D, D]), din("d_wo", [D, D])]
    a_q_norm = din("a_q_norm", [1, 128])
    a_k_norm = din("a_k_norm", [1, 128])
    b_sink = din("b_sink", [1, 16])
    c_lambda = din("c_lambda", [4, 128])
    c_subln = din("c_subln", [2, 128])
    w_up = din("ffn_w_up", [DEPTH, D, 2 * DFF])
    conv_w = din("ffn_conv_w", [DEPTH, 3, FC, 128])
    conv_b = din("ffn_conv_b", [DEPTH, FC, 128])
    w_down = din("ffn_w_down", [DEPTH, DFF, D])
    norm_final = din("norm_final", [KC, 128])
    tb_d = din("tb", [16, 128, TB_L])
    td_d = din("td", [16, 128, TD_L])
    tc_d = din("tc", [8, 128, TC_L])
    rope_c_d = din("rope_c", [128, 4096])
    rope_s_d = din("rope_s", [128, 4096])
    pm_d = din("pm", [128, 128])
    ident_d = din("ident", [128, 128])

    XT = [dscr("XT%d" % s, [KC, 128, S], F32) for s, S in enumerate(S_list)]
    HT = [dscr("HT%d" % s, [KC, 128, S], BF16) for s, S in enumerate(S_list)]
    QT = [dscr("QT%d" % s, [16, 128, S], BF16) for s, S in enumerate(S_list)]
    KT = [dscr("KT%d" % s, [8, 128, S], BF16) for s, S in enumerate(S_list)]
    VV = [dscr("VV%d" % s, [S, 1024], BF16) for s, S in enumerate(S_list)]
    OT = [dscr("OT%d" % s, [16, 128, S], BF16) for s, S in enumerate(S_list)]
    GT = [dscr("GT%d" % s, [FC, 128, S], BF16) for s, S in enumerate(S_list)]

    def sb(name, shape, dt):
        return es.enter_context(nc.sbuf_tensor(name, list(shape), dt))

    WB = [sb("WB0", [128, 24576], BF16), sb("WB1", [128, 24576], BF16)]
    RA = [sb("RA0", [128, 8192], BF16), sb("RA1", [128, 8192], BF16)]
    RB = sb("RB", [128, 11264], F32)
    NF32 = 6
    NB16 = 6
    RSTD = [sb("RSTD%d" % i, [128, 256], F32) for i in range(2)]
    QTILE = sb("QTILE", [128, 1024], BF16)
    SF = [sb("SF%d" % i, [128, 512], F32) for i in range(NF32)]
    SH = [sb("SH%d" % i, [128, 512], BF16) for i in range(NB16)]
    IDENT = sb("IDENT", [128, 128], F32)
    ONESB = sb("ONESB", [128, 128], BF16)
    ONESF = sb("ONESF", [128, 128], F32)
    PMB = sb("PMB", [128, 128], BF16)
    PMF = sb("PMF", [128, 128], F32)
    MODP = sb("MODP", [128, NSEQ * DEPTH * 6 * KC], F32)
    MODRAW = sb("MODRAW", [128, 96 * NSEQ], F32)
    BADA = sb("BADA", [128, 96], F32)
    NAT = sb("NAT", [128, 2 * KC], F32)
    SCT = sb("SCT", [128, KC * NSEQ], F32)
    CT = sb("CT", [128, KC * NSEQ], F32)
    SCTB = sb("SCTB", [128, KC * NSEQ], BF16)
    CONVW = sb("CONVW", [128, DEPTH * 3 * FC], F32)
    CONVB = sb("CONVB", [128, DEPTH * FC], F32)
    NFIN = sb("NFIN", [128, KC], F32)
    SMALL = sb("SMALL", [128, 64], F32)
    ROWS = sb("ROWS", [128, 128], F32)
    PS = [es.enter_context(nc.psum_tensor("PS%d" % i, [128, 512], F32)) for i in range(8)]

    sch = Sched(nc, es)
    ctr = {"sf": 0, "sh": 0, "ps": 0, "ev": 0}

    def sf():
        i = ctr["sf"]
        ctr["sf"] = (i + 1) % NF32
        return SF[i], ("SF", i)

    def sh():
        i = ctr["sh"]
        ctr["sh"] = (i + 1) % NB16
        return SH[i], ("SH", i)

    def psb(lo=0, hi=8):
        i = ctr["ps"]
        if i < lo or i >= hi:
            i = lo
        ctr["ps"] = i + 1 if i + 1 < hi else lo
        return PS[i], ("PS", i)

    slot_ctr = {"RA": 0, "RB": 0}

    def next_slot(name):
        u = slot_ctr[name]
        slot_ctr[name] ^= 1
        return u

    def evac_eng():
        ctr["ev"] ^= 1
        return "act" if ctr["ev"] else "dve"

    def copy_op(eng, out, in_, reads, writes):
        if eng == "act":
            return sch.op("act", lambda e: e.activation(out=out, in_=in_, func=AF.Copy), reads, writes)
        return sch.op(eng, lambda e: e.tensor_copy(out, in_), reads, writes)

    def mm(out, lhsT, rhs, start, stop, reads, writes, force=False):
        return sch.op("pe", lambda e: e.matmul(out, lhsT, rhs, start=start, stop=stop), reads, writes, signal=(stop or force))

    sch.dma("sp", IDENT[:], ident_d, writes=("IDENT",))
    sch.dma("sp", PMF[:], pm_d, writes=("PMF",))
    sch.op("dve", lambda e: e.tensor_copy(PMB[:], PMF[:]), reads=("PMF",), writes=("PMB",))
    sch.op("dve", lambda e: e.memset(ONESB[:], 1.0), writes=("ONESB",))
    sch.op("dve", lambda e: e.memset(ONESF[:], 1.0), writes=("ONESF",))

    def load_T(src_rows, nrows, dst, dst_key):
        sch.dma("sp", ROWS[0:nrows, :], src_rows, writes=("ROWS",))
        ps, pk = psb()
        sch.op("pe", lambda e: e.transpose(ps[:, 0:nrows], ROWS[0:nrows, :], IDENT[0:nrows, 0:nrows]),
               reads=("ROWS", "IDENT"), writes=(pk,))
        sch.op("dve", lambda e: e.tensor_copy(dst, ps[:, 0:nrows]), reads=(pk,), writes=(dst_key,))

    load_T(norm_final, KC, NFIN[:, :], "NFIN")
    load_T(a_q_norm, 1, SMALL[:, 0:1], "SMALL_QN")
    load_T(a_k_norm, 1, SMALL[:, 1:2], "SMALL_KN")
    load_T(c_subln, 2, SMALL[:, 2:4], "SMALL_SUBW")
    sch.op("dve", lambda e: e.tensor_scalar(SMALL[:, 2:4], SMALL[:, 2:4], 1.0 - _lambda_init(2), None, op0=ALU.mult),
           reads=("SMALL_SUBW",), writes=("SMALL_SUBW",))
    load_T(c_lambda, 4, SMALL[:, 24:28], "SMALL_LT")
    sch.op("dve", lambda e: e.tensor_tensor(SMALL[:, 28:29], SMALL[:, 24:25], SMALL[:, 25:26], ALU.mult),
           reads=("SMALL_LT",), writes=("SMALL_LP0",))
    sch.op("dve", lambda e: e.tensor_tensor(SMALL[:, 29:30], SMALL[:, 26:27], SMALL[:, 27:28], ALU.mult),
           reads=("SMALL_LT",), writes=("SMALL_LP1",))
    ps, pk = psb()
    mm(ps[:, 0:2], ONESF[:], SMALL[:, 28:30], True, True, reads=("ONESF", "SMALL_LP0", "SMALL_LP1"), writes=(pk,))
    sch.op("act", lambda e, ps=ps: e.activation(out=SMALL[:, 30:32], in_=ps[:, 0:2], func=AF.Exp),
           reads=(pk,), writes=("SMALL_LE",))
    sch.op("dve", lambda e: e.tensor_tensor(SMALL[:, 4:5], SMALL[:, 31:32], SMALL[:, 30:31], ALU.subtract),
           reads=("SMALL_LE",), writes=("SMALL_NL",))
    sch.op("dve", lambda e: e.tensor_scalar(SMALL[:, 4:5], SMALL[:, 4:5], -_lambda_init(2), None, op0=ALU.add),
           reads=("SMALL_NL",), writes=("SMALL_NL",))
    sch.dma("sp", SMALL[:, 8:24], b_sink.partition_broadcast(128), writes=("SMALL_ES",))
    sch.op("act", lambda e: e.activation(out=SMALL[:, 8:24], in_=SMALL[:, 8:24], func=AF.Exp),
           reads=("SMALL_ES",), writes=("SMALL_ES",))
    for l in range(depth):
        for j in range(3):
            load_T(conv_w[l, j], FC, CONVW[:, (l * 3 + j) * FC:(l * 3 + j + 1) * FC], ("CONVW", l, j))
        load_T(conv_b[l], FC, CONVB[:, l * FC:(l + 1) * FC], ("CONVB", l))

    for s in range(NSEQ):
        load_T(c_in[s], KC, CT[:, s * KC:(s + 1) * KC], ("CT", s))
    sch.op("act", lambda e: e.activation(out=SCT[:, :], in_=CT[:, :], func=AF.Sigmoid),
           reads=[("CT", s) for s in range(NSEQ)], writes=("SCT",))
    sch.op("dve", lambda e: e.tensor_tensor(SCT[:, :], SCT[:, :], CT[:, :], ALU.mult),
           reads=[("CT", s) for s in range(NSEQ)] + ["SCT"], writes=("SCT",))
    sch.op("dve", lambda e: e.tensor_copy(SCTB[:, :], SCT[:, :]), reads=("SCT",), writes=("SCTB",))
    SCTv = SCTB[:, :].rearrange("p (s k) -> p k s", s=NSEQ)
    ADA_NC = 1536

    def modp(s, l, which):
        o = ((s * DEPTH + l) * 6 + which) * KC
        return MODP[:, o:o + KC]

    slab = 0
    for l in range(depth):
        load_T(b_ada[l], 96, BADA[:, :], "BADA")
        load_T(norm_attn[l], KC, NAT[:, 0:KC], "NAT0")
        load_T(norm_ffn[l], KC, NAT[:, KC:2 * KC], "NAT1")
        ps, pk = psb()
        for sl in range(6 * D // ADA_NC):
            h = slab % 2
            slab += 1
            wv = WB[h][:, 0:KC * ADA_NC].rearrange("p (k n) -> p k n", k=KC)
            for g in range(4):
                sch.dma("pool", wv[:, g * 4:(g + 1) * 4, :],
                        w_ada[l, g * 512:(g + 1) * 512, sl * ADA_NC:(sl + 1) * ADA_NC].rearrange("(k p) n -> p k n", p=128),
                        writes=(("WB", h, g),))
            for j in range(ADA_NC // 128):
                oc = sl * (ADA_NC // 128) + j
                for kc in range(KC):
                    mm(ps[:, oc * NSEQ:(oc + 1) * NSEQ], wv[:, kc, j * 128:(j + 1) * 128], SCTv[:, kc, :],
                       kc == 0, kc == KC - 1, reads=(("WB", h, kc // 4), "SCTB"), writes=(pk,))
        for s in range(NSEQ):
            sch.op("dve", lambda e, s=s, ps=ps: e.tensor_tensor(
                MODRAW[:, s * 96:(s + 1) * 96], ps[:, 0:96 * NSEQ].rearrange("p (o s) -> p s o", s=NSEQ)[:, s, :],
                BADA[:, :], ALU.add), reads=(pk, "BADA"), writes=(("MODRAW", s),))
            R = lambda w, s=s: MODRAW[:, s * 96 + w * KC: s * 96 + (w + 1) * KC]
            for (dst, a_src, nat) in ((0, 1, 0), (3, 4, 1)):
                sch.op("dve", lambda e, dst=dst, a_src=a_src, nat=nat, s=s, l=l, R=R: e.scalar_tensor_tensor(
                    modp(s, l, dst), R(a_src), 1.0, NAT[:, nat * KC:(nat + 1) * KC], ALU.add, ALU.mult),
                    reads=(("MODRAW", s), "NAT0", "NAT1"), writes=(("MODP", s, l, dst),))
            for (dst, src) in ((1, 0), (2, 2), (4, 3), (5, 5)):
                sch.op("dve", lambda e, dst=dst, src=src, s=s, l=l, R=R: e.tensor_copy(modp(s, l, dst), R(src)),
                       reads=(("MODRAW", s),), writes=(("MODP", s, l, dst),))

    sch.op("dve", lambda e: e.memset(SMALL[:, 40:41], 0.0),
           reads=[("MODP", s_, l_, w_) for s_ in range(NSEQ) for l_ in range(depth) for w_ in range(6)], writes=("MODPALL",))

    def wb_view(h, kcn, ncols):
        return WB[h][:, 0:kcn * ncols].rearrange("p (k n) -> p k n", k=kcn)

    def load_w(h, src2d, k0, kcn, col_ranges):
        ncols = sum(c1 - c0 for c0, c1 in col_ranges)
        wv = wb_view(h, kcn, ncols)
        step = 4
        for g0 in range(0, kcn, step):
            g1 = min(kcn, g0 + step)
            o = 0
            for c0, c1 in col_ranges:
                sch.dma("pool", wv[:, g0:g1, o:o + (c1 - c0)],
                        src2d[(k0 + g0) * 128:(k0 + g1) * 128, c0:c1].rearrange("(k p) n -> p k n", p=128),
                        writes=(("WB", h, g0 // step), ("WB", h, "att", 0), ("WB", h, "att", 1)))
                o += c1 - c0
        return wv

    def wkeys(h, kc, col_ranges):
        return [("WB", h, kc // 4)]

    wb_next = [0]

    def next_wb():
        h = wb_next[0]
        wb_next[0] ^= 1
        return h

    def rstd_from_ps(ps, pk, T, n_feat, dst=None):
        r, rk = dst if dst is not None else sf()
        sch.op("act", lambda e: e.activation(out=r[:, 0:T], in_=ps[:, 0:T], func=AF.Sqrt, bias=EPS, scale=1.0 / n_feat),
               reads=(pk,), writes=(rk,))
        sch.op("dve", lambda e: e.reciprocal(r[:, 0:T], r[:, 0:T]), reads=(rk,), writes=(rk,))
        return r, rk

    def phase_transpose_in(s):
        S = S_list[s]
        T = 256
        RBv = RB[:, :].rearrange("p (u f) -> p u f", u=2)[:, :, 0:4096].rearrange("p u (b f) -> p u b f", b=2)
        nt = S // T
        us = [next_slot("RB") for _ in range(nt)]

        def load(ti):
            for b in range(2):
                sch.dma("sp", RBv[:, us[ti], b, :], x_in[s][ti * T + b * 128:ti * T + (b + 1) * 128, :], writes=(("RB", us[ti]),))

        load(0)
        for ti in range(nt):
            u = us[ti]
            t0 = ti * T
            if ti + 1 < nt:
                load(ti + 1)
            for kc in range(KC):
                ps, pk = psb()
                for b in range(2):
                    sch.op("pe", lambda e, ps=ps, b=b, u=u, kc=kc: e.transpose(
                        ps[:, b * 128:(b + 1) * 128], RBv[:, u, b, kc * 128:(kc + 1) * 128], IDENT[:, :]),
                        reads=(("RB", u), "IDENT"), writes=(pk,), signal=(b == 1))
                st, sk = sf()
                copy_op(evac_eng(), st[:, 0:T], ps[:, 0:T], (pk,), (sk,))
                sch.dma(STQ, XT[s][kc, :, t0:t0 + T], st[:, 0:T], reads=(sk,), writes=(("XT", s, kc),))

    def phase_norm(s, A_ap, B_ap, final=False):
        S = S_list[s]
        T = 256
        RBv = RB[:, :].rearrange("p (u f) -> p u f", u=2)[:, :, 0:4096].rearrange("p u (k t) -> p u k t", k=KC)
        nt = S // T
        us = [next_slot("RB") for _ in range(nt)]
        uas = [next_slot("RA") for _ in range(nt)]

        def load(ti):
            sch.dma("sp", RBv[:, us[ti], :, :], XT[s][:, :, ti * T:(ti + 1) * T].rearrange("k p t -> p k t"),
                    reads=[("XT", s, kc) for kc in range(KC)], writes=(("RB", us[ti]),))

        load(0)
        for ti in range(nt):
            u = us[ti]
            ua = uas[ti]
            t0 = ti * T
            if ti + 1 < nt:
                load(ti + 1)
            ps, pk = psb()
            for kc in range(KC):
                q, qk = sh()
                sch.op("act", lambda e, q=q, u=u, kc=kc: e.activation(out=q[:, 0:T], in_=RBv[:, u, kc, :], func=AF.Square),
                       reads=(("RB", u),), writes=(qk,))
                mm(ps[:, 0:T], ONESB[:, :], q[:, 0:T], kc == 0, kc == KC - 1, reads=(qk, "ONESB"), writes=(pk,), force=True)
            r, rk = rstd_from_ps(ps, pk, T, D, dst=(RSTD[u], ("RSTD", u)))
            if not final:
                hout = RA[ua][:, 0:KC * T].rearrange("p (k t) -> p k t", k=KC)
                for kc in range(KC):
                    tm, tk = sf()
                    sch.op("dve", lambda e, tm=tm, u=u, kc=kc, r=r: e.tensor_tensor(tm[:, 0:T], RBv[:, u, kc, :], r[:, 0:T], ALU.mult),
                           reads=(("RB", u), rk), writes=(tk,))
                    sch.op("act", lambda e, tm=tm, kc=kc, hout=hout: e.activation(
                        out=hout[:, kc, :], in_=tm[:, 0:T], func=AF.Identity, bias=B_ap[:, kc:kc + 1], scale=A_ap[:, kc:kc + 1]),
                        reads=(tk, "MODPALL"), writes=(("RA", ua),))
                sch.dma(STQ, HT[s][:, :, t0:t0 + T].rearrange("k p t -> p k t"), hout[:, :, :],
                        reads=(("RA", ua),), writes=(("HT", s),))
            else:
                for b in range(T // 128):
                    orow = RA[ua][:, b * 4096:(b + 1) * 4096].bitcast(F32)
                    okey = ("RAo", ua, b)
                    for kc in range(KC):
                        tm, tk = sf()
                        sch.op("dve", lambda e, tm=tm, u=u, kc=kc, r=r, b=b: e.scalar_tensor_tensor(
                            tm[:, 0:128], RBv[:, u, kc, b * 128:(b + 1) * 128], A_ap[:, kc:kc + 1], r[:, b * 128:(b + 1) * 128],
                            ALU.mult, ALU.mult), reads=(("RB", u), rk, "NFIN"), writes=(tk,))
                        if kc % 4 == 0:
                            ps2, pk2 = psb()
                        sch.op("pe", lambda e, ps2=ps2, tm=tm, kc=kc: e.transpose(
                            ps2[:, (kc % 4) * 128:(kc % 4 + 1) * 128], tm[:, 0:128], IDENT[:, :]),
                            reads=(tk, "IDENT"), writes=(pk2,), signal=True)
                        if kc % 4 == 3:
                            copy_op(evac_eng(), orow[:, (kc - 3) * 128:(kc + 1) * 128], ps2[:, :], (pk2,), (okey, ("RA", ua)))
                    sch.dma(STQ, y_out[s][t0 + b * 128:t0 + (b + 1) * 128, :], orow[:, :],
                            reads=(okey,), writes=(("Y", s),))

    def linear_pass(s, src, src_key, kcn, wv, h, col_ranges, n_oc, epilogue, T, tiles=None, in_buf="RA", v_token_major=None,
                    pre_tile=None, oc_order=None):
        S = S_list[s]
        if tiles is None:
            tiles = [(t0, T, t0, T, 0) for t0 in range(0, S, T)]
        if oc_order is None:
            oc_order = list(range(n_oc))

        us = [next_slot(in_buf) for _ in tiles]

        def bufof(ti):
            u = us[ti]
            if in_buf == "RA":
                return RA[u][:, 0:kcn * T].rearrange("p (k t) -> p k t", k=kcn), ("RA", u)
            return RB[:, :].bitcast(BF16)[:, u * 11264:u * 11264 + kcn * T].rearrange("p (k t) -> p k t", k=kcn), ("RB", u)

        def load(ti):
            lo, n, olo, on, coff = tiles[ti]
            buf, bkey = bufof(ti)
            vlo, vhi = max(lo, 0), min(lo + n, S)
            if vlo > lo or vhi < lo + n:
                sch.op("dve", lambda e, buf=buf, n=n: e.memset(buf[:, :, 0:n], 0.0), writes=(bkey,))
            sch.dma("sp", buf[:, :, vlo - lo:vhi - lo], src[:, :, vlo:vhi].rearrange("k p t -> p k t"),
                    reads=(src_key,), writes=(bkey,))
            if pre_tile is not None:
                pre_tile(ti, tiles[ti])

        load(0)
        for ti, (lo, n, olo, on, coff) in enumerate(tiles):
            if ti + 1 < len(tiles):
                load(ti + 1)
            buf, bkey = bufof(ti)
            rd_in = (bkey,)
            for oc in oc_order:
                ps, pk = psb()
                for kc in range(kcn):
                    mm(ps[:, 0:n], wv[:, kc, oc * 128:(oc + 1) * 128], buf[:, kc, 0:n], kc == 0, kc == kcn - 1,
                       reads=tuple(wkeys(h, kc, col_ranges)) + rd_in, writes=(pk,))
                epilogue(oc, ps, pk, (lo, n, olo, on, coff), ti)
            if v_token_major is not None:
                vc0, vn, vdst0 = v_token_major
                for tb in range(n // 128):
                    for c in range(0, vn, 512):
                        cw = min(512, vn - c)
                        ps, pk = psb()
                        for kc in range(kcn):
                            mm(ps[:, 0:cw], buf[:, kc, tb * 128:(tb + 1) * 128], wv[:, kc, vc0 + c:vc0 + c + cw], kc == 0, kc == kcn - 1,
                               reads=tuple(wkeys(h, kc, col_ranges)) + rd_in, writes=(pk,))
                        st, sk = sh()
                        copy_op(evac_eng(), st[:, 0:cw], ps[:, 0:cw], (pk,), (sk,))
                        sch.dma(STQ, VV[s][lo + tb * 128:lo + (tb + 1) * 128, vdst0 + c:vdst0 + c + cw], st[:, 0:cw],
                                reads=(sk,), writes=(("VV", s),))

    def phase_qkv(s, l):
        m = l % 4
        S = S_list[s]
        T = 512
        w = wqkv[m]
        if m == 2:
            nq, nk, nv = 16, 8, 1024
        else:
            nq, nk, nv = 16, 4, 512
        fm_chunks = [("q", i) for i in range(nq)] + [("k", i) for i in range(nk)]
        groups = [fm_chunks[i:i + 12] for i in range(0, len(fm_chunks), 12)]
        vcol0 = (nq + nk) * 128
        plans = []
        for gi, g in enumerate(groups):
            cols = [(((0 if kind == "q" else nq) + i) * 128) for kind, i in g]
            cr = [(cols[0], cols[-1] + 128)]
            vt = None
            if gi == len(groups) - 1 and (len(g) * 128 + nv) * KC <= 24576:
                cr = [(cols[0], cols[-1] + 128 + nv)]
                vt = (len(g) * 128, nv, 0)
            plans.append((g, cr, vt))
        if plans[-1][2] is None:
            plans.append(([], [(vcol0, vcol0 + nv)], (0, nv, 0)))

        def make_epi(g):
            def epi(oc, ps, pk, tinfo, ti):
                lo, n, olo, on, coff = tinfo
                kind, i = g[oc]
                ROPEC, ROPES, rkey = rope_bufs(ti)
                dst = QT[s][i] if kind == "q" else KT[s][i]
                dkey = ("QT", s) if kind == "q" else ("KT", s)
                if m != 0:
                    st, sk = sh()
                    copy_op(evac_eng(), st[:, 0:n], ps[:, 0:n], (pk,), (sk,))
                    sch.dma(STQ, dst[:, lo:lo + n], st[:, 0:n], reads=(sk,), writes=(dkey,))
                    return
                gcol = SMALL[:, 0:1] if kind == "q" else SMALL[:, 1:2]
                sq, sqk = sh()
                sch.op("act", lambda e: e.activation(out=sq[:, 0:n], in_=ps[:, 0:n], func=AF.Square), reads=(pk,), writes=(sqk,))
                ps2, pk2 = psb()
                mm(ps2[:, 0:n], ONESB[:, :], sq[:, 0:n], True, True, reads=(sqk, "ONESB"), writes=(pk2,))
                r, rk = rstd_from_ps(ps2, pk2, n, HD)
                qn, qnk = sf()
                sch.op("dve", lambda e: e.scalar_tensor_tensor(qn[:, 0:n], ps[:, 0:n], gcol, r[:, 0:n], ALU.mult, ALU.mult),
                       reads=(pk, rk, "SMALL_QN", "SMALL_KN"), writes=(qnk,))
                qb, qbk = sh()
                sch.op("act", lambda e: e.activation(out=qb[:, 0:n], in_=qn[:, 0:n], func=AF.Copy), reads=(qnk,), writes=(qbk,))
                ps3, pk3 = psb()
                mm(ps3[:, 0:n], PMB[:, :], qb[:, 0:n], True, True, reads=(qbk, "PMB"), writes=(pk3,))
                t1, t1k = sf()
                sch.op("dve", lambda e: e.tensor_tensor(t1[:, 0:n], qn[:, 0:n], ROPEC[:, 0:n], ALU.mult),
                       reads=(qnk, rkey), writes=(t1k,))
                t2, t2k = sf()
                sch.op("dve", lambda e: e.tensor_tensor(t2[:, 0:n], ps3[:, 0:n], ROPES[:, 0:n], ALU.mult),
                       reads=(pk3, rkey), writes=(t2k,))
                st, sk = sh()
                sch.op("dve", lambda e: e.tensor_tensor(st[:, 0:n], t1[:, 0:n], t2[:, 0:n], ALU.add),
                       reads=(t1k, t2k), writes=(sk,))
                sch.dma(STQ, dst[:, lo:lo + n], st[:, 0:n], reads=(sk,), writes=(dkey,))
            return epi

        wvs = [None] * len(plans)
        pf = hoisted.pop(("qkv", s, l), None)
        if pf is not None and plans[0][1] == [(0, 1536)]:
            hs = [pf[0]] + [next_wb() for _ in plans[1:]]
            wvs[0] = pf[1]
        else:
            hs = [next_wb() for _ in plans]
            wvs[0] = load_w(hs[0], w, 0, KC, plans[0][1])
        for pi, (g, cr, vt) in enumerate(plans):
            if pi + 1 < len(plans):
                wvs[pi + 1] = load_w(hs[pi + 1], w, 0, KC, plans[pi + 1][1])
            pre = None
            if m == 0 and len(g):
                def pre(ti, tinfo):
                    c_, s_, rkey = rope_bufs(ti)
                    sch.dma("sp", c_[:, 0:T], rope_c_d[:, tinfo[0]:tinfo[0] + T], writes=(rkey,))
                    sch.dma("sp", s_[:, 0:T], rope_s_d[:, tinfo[0]:tinfo[0] + T], writes=(rkey,))
            linear_pass(s, HT[s], ("HT", s), KC, wvs[pi], hs[pi], cr, len(g), make_epi(g), T, v_token_major=vt, pre_tile=pre)

    def rope_bufs(ti):
        u = ti % 2
        return RB[:, u * 5632:u * 5632 + 512], RB[:, u * 5632 + 512:u * 5632 + 1024], ("RB", u)

    def phase_attn(s, l):
        m = l % 4
        S = S_list[s]
        NQ = 512
        TABSINGLE = os.environ.get("K_TABSINGLE", "0") == "1"
        LA = int(os.environ.get("K_LA", "3"))
        nblk = S // 128
        M = 2 if m == 2 else 1
        DV = 2 if m == 2 else 1
        G = 2 if m == 2 else 4
        table = {0: None, 1: (tb_d, TB_CEN, TB_L), 2: (tc_d, TC_CEN, TC_L), 3: (td_d, TD_CEN, TD_L)}[m]
        KB16 = WB[0]
        VB16 = WB[1]
        na = DV + 1
        if m == 2:
            acc_sets = [0]
            rot_lo = na
        else:
            acc_sets = [0, na]
            rot_lo = 2 * na
        allw0 = [("WB", 0, i) for i in range(11)]
        allw1 = [("WB", 1, i) for i in range(11)]
        U0 = RA[0][:, 0:2048].bitcast(F32)

        def kv_views(kvh):
            par = kvh % 2
            kview = KB16[:, par * M * S:(par + 1) * M * S].rearrange("p (m t) -> p m t", m=M)
            vview = VB16[:, par * nblk * DV * 128:(par + 1) * nblk * DV * 128].rearrange("p (b d) -> p b d", b=nblk)
            return kview, vview, ("WB", 0, "att", par), ("WB", 1, "att", par)

        def load_kv(kvh):
            kview, vview, kkey, vkey = kv_views(kvh)
            for mm_ in range(M):
                sch.dma("sp", kview[:, mm_, :], KT[s][kvh * M + mm_], reads=(("KT", s),), writes=tuple([kkey] + allw0))
            for b0 in range(0, nblk, 8):
                b1 = min(nblk, b0 + 8)
                sch.dma("sp", vview[:, b0:b1, :],
                        VV[s][b0 * 128:b1 * 128, kvh * DV * 128:(kvh + 1) * DV * 128].rearrange("(b p) d -> p b d", p=128),
                        reads=(("VV", s),), writes=tuple([vkey] + allw1))

        items = [(kvh, g, qi, mm_) for kvh in range(4) for g in range(G) for qi in range(S // NQ) for mm_ in range(M)]

        def qslot(ii):
            return ii % 2

        def load_q(ii):
            kvh, g, qi, mm_ = items[ii]
            h = kvh * G + g
            sl = qslot(ii)
            sch.dma("sp", QTILE[:, sl * 512:(sl + 1) * 512], QT[s][h * M + mm_][:, qi * NQ:(qi + 1) * NQ],
                    reads=(("QT", s),), writes=(("QTILE", sl),))

        def tab_view(h):
            tab, cen, L = table
            if m == 2 or TABSINGLE:
                return RB[:, 0:L], (("RB", 0), ("RB", 1))
            u = h % 2
            return RB[:, u * 5632:u * 5632 + L], (("RB", u),)

        def load_tab(h):
            tab, cen, L = table
            tv, tk = tab_view(h)
            sch.dma("sp", tv, tab[h], writes=tk)

        load_kv(0)
        if table is not None:
            load_tab(0)
        load_q(0)
        pass_i = 0
        for ii, (kvh, g, qi, mm_) in enumerate(items):
            h = kvh * G + g
            kview, vview, kkey, vkey = kv_views(kvh)
            first_of_kvh = (g == 0 and qi == 0 and mm_ == 0)
            first_of_head = (qi == 0 and mm_ == 0)
            if first_of_kvh and kvh + 1 < 4:
                load_kv(kvh + 1)
            if first_of_head and table is not None and h + 1 < 4 * G and m != 2 and not TABSINGLE:
                load_tab(h + 1)
            if first_of_head and table is not None and (m == 2 or TABSINGLE) and h > 0:
                load_tab(h)
            if ii + 1 < len(items):
                load_q(ii + 1)
            q0 = qi * NQ
            qb0 = q0 // 128
            if m == 1:
                kbs = [kb for kb in range(qb0 - 1, qb0 + 5) if 0 <= kb < nblk]
            elif m == 3:
                kbs = [kb for kb in range(qb0 - 8, qb0 + 12) if 0 <= kb < nblk]
            else:
                kbs = list(range(nblk))
            sl = qslot(ii)
            qt_, qk_ = QTILE[:, sl * 512:(sl + 1) * 512], ("QTILE", sl)
            base = acc_sets[pass_i % len(acc_sets)]
            pass_i += 1
            accO = [(PS[base + dv], ("PS", base + dv)) for dv in range(DV)]
            accZ = (PS[base + DV], ("PS", base + DV))
            if table is not None:
                tv, tkeys = tab_view(h)
                cen = table[1]
            pt = {}

            def stage1(idx):
                kb = kbs[idx]
                ps, pk = psb(rot_lo, 8)
                mm(ps[:, 0:NQ], kview[:, mm_, kb * 128:(kb + 1) * 128], qt_[:, 0:NQ], True, True, reads=(kkey, qk_), writes=(pk,))
                p_, pkk = sh()
                if table is None:
                    sch.op("act", lambda e, p_=p_, ps=ps: e.activation(out=p_[:, 0:NQ], in_=ps[:, 0:NQ], func=AF.Exp, scale=SCALE),
                           reads=(pk,), writes=(pkk,))
                else:
                    e_, ek = sf()
                    sch.op("act", lambda e, e_=e_, ps=ps: e.activation(out=e_[:, 0:NQ], in_=ps[:, 0:NQ], func=AF.Exp, scale=SCALE),
                           reads=(pk,), writes=(ek,))
                    c0 = cen - (kb * 128 - q0)
                    sch.op("dve", lambda e, p_=p_, e_=e_, c0=c0, tv=tv: e.tensor_tensor(p_[:, 0:NQ], e_[:, 0:NQ], tv[:, c0:c0 + NQ], ALU.mult),
                           reads=(ek,) + tkeys, writes=(pkk,))
                pt[idx] = (p_, pkk)

            def stage2(idx):
                kb = kbs[idx]
                p_, pkk = pt.pop(idx)
                first, last = idx == 0, idx == len(kbs) - 1
                for dv in range(DV):
                    mm(accO[dv][0][:, 0:NQ], vview[:, kb, dv * 128:(dv + 1) * 128], p_[:, 0:NQ], first, last,
                       reads=(vkey, pkk), writes=(accO[dv][1],))
                mm(accZ[0][:, 0:NQ], ONESB[:, :], p_[:, 0:NQ], first, last, reads=("ONESB", pkk), writes=(accZ[1],), force=True)

            nk = len(kbs)
            for j in range(nk + LA):
                if j < nk:
                    stage1(j)
                if j >= LA:
                    stage2(j - LA)
            z, zk = sf()
            zp, zpk = accZ
            if m == 1:
                sch.op("dve", lambda e, z=z, zp=zp, h=h: e.tensor_scalar(z[:, 0:NQ], zp[:, 0:NQ], SMALL[:, 8 + h:9 + h], None, op0=ALU.add),
                       reads=(zpk, "SMALL_ES"), writes=(zk,))
                sch.op("dve", lambda e, z=z: e.reciprocal(z[:, 0:NQ], z[:, 0:NQ]), reads=(zk,), writes=(zk,))
            else:
                sch.op("dve", lambda e, z=z, zp=zp: e.reciprocal(z[:, 0:NQ], zp[:, 0:NQ]), reads=(zpk,), writes=(zk,))
            if m != 2:
                o_, ok = sh()
                op_, opk = accO[0]
                sch.op("dve", lambda e, o_=o_, op_=op_, z=z: e.tensor_tensor(o_[:, 0:NQ], op_[:, 0:NQ], z[:, 0:NQ], ALU.mult),
                       reads=(opk, zk), writes=(ok,))
                sch.dma(STQ, OT[s][h][:, q0:q0 + NQ], o_[:, 0:NQ], reads=(ok,), writes=(("OT", s),))
            elif mm_ == 0:
                for dv in range(DV):
                    sch.op("dve", lambda e, dv=dv, a=accO[dv][0], z=z: e.tensor_tensor(U0[:, dv * 512:dv * 512 + NQ], a[:, 0:NQ], z[:, 0:NQ], ALU.mult),
                           reads=(accO[dv][1], zk), writes=(("RA", 0),))
            else:
                psn, pnk = psb(rot_lo, 8)
                for dv in range(DV):
                    v_, vk = sf()
                    sch.op("dve", lambda e, v_=v_, a=accO[dv][0], z=z: e.tensor_tensor(v_[:, 0:NQ], a[:, 0:NQ], z[:, 0:NQ], ALU.mult),
                           reads=(accO[dv][1], zk), writes=(vk,))
                    sch.op("dve", lambda e, v_=v_, dv=dv: e.scalar_tensor_tensor(
                        U0[:, dv * 512:dv * 512 + NQ], v_[:, 0:NQ], SMALL[:, 4:5], U0[:, dv * 512:dv * 512 + NQ], ALU.mult, ALU.add),
                        reads=(vk, "SMALL_NL", ("RA", 0)), writes=(("RA", 0),))
                    q_, qk2 = sh()
                    sch.op("act", lambda e, q_=q_, dv=dv: e.activation(out=q_[:, 0:NQ], in_=U0[:, dv * 512:dv * 512 + NQ], func=AF.Square),
                           reads=(("RA", 0),), writes=(qk2,))
                    mm(psn[:, 0:NQ], ONESB[:, :], q_[:, 0:NQ], dv == 0, dv == DV - 1, reads=(qk2, "ONESB"), writes=(pnk,), force=True)
                r, rk = rstd_from_ps(psn, pnk, NQ, 2 * HD)
                for dv in range(DV):
                    o_, ok = sh()
                    sch.op("dve", lambda e, o_=o_, dv=dv, r=r: e.scalar_tensor_tensor(
                        o_[:, 0:NQ], U0[:, dv * 512:dv * 512 + NQ], SMALL[:, 2 + dv:3 + dv], r[:, 0:NQ], ALU.mult, ALU.mult),
                        reads=(("RA", 0), rk, "SMALL_SUBW"), writes=(ok,))
                    sch.dma(STQ, OT[s][h * 2 + dv][:, q0:q0 + NQ], o_[:, 0:NQ], reads=(ok,), writes=(("OT", s),))

    def phase_resid(s, src, src_key, kcn, w2d, gate_ap, T, oc_per, in_buf):
        S = S_list[s]
        n_sub = KC // oc_per
        hs = [next_wb() for _ in range(n_sub)]
        wvs = [None] * n_sub
        wvs[0] = load_w(hs[0], w2d, 0, kcn, [(0, oc_per * 128)])
        for si in range(n_sub):
            if si + 1 < n_sub:
                wvs[si + 1] = load_w(hs[si + 1], w2d, 0, kcn, [((si + 1) * oc_per * 128, (si + 2) * oc_per * 128)])
            cr = [(si * oc_per * 128, (si + 1) * oc_per * 128)]

            def epi(oc, ps, pk, tinfo, ti, si=si):
                lo, n, olo, on, coff = tinfo
                ocg = si * oc_per + oc
                xc, xk = sf()
                sch.dma("sp", xc[:, 0:n], XT[s][ocg, :, lo:lo + n], reads=(("XT", s, ocg),), writes=(xk,))
                sch.op("dve", lambda e: e.scalar_tensor_tensor(xc[:, 0:n], ps[:, 0:n], gate_ap[:, ocg:ocg + 1], xc[:, 0:n], ALU.mult, ALU.add),
                       reads=(pk, xk, "MODPALL"), writes=(xk,))
                sch.dma(STQ, XT[s][ocg, :, lo:lo + n], xc[:, 0:n], reads=(xk,), writes=(("XT", s, ocg),))

            linear_pass(s, src, src_key, kcn, wvs[si], hs[si], cr, oc_per, epi, T, in_buf=in_buf)

    def phase_ffn_up(s, l):
        S = S_list[s]
        T = 512
        step = T - 2
        tiles = []
        t0 = 0
        while t0 < S:
            on = min(step, S - t0)
            tiles.append((t0 - 1, on + 2, t0, on, 1))
            t0 += step
        subs = [list(range(i, min(i + 6, FC))) for i in range(0, FC, 6)]
        wvs = [None] * len(subs)

        def crs(fcs):
            return [(fcs[0] * 128, (fcs[-1] + 1) * 128), (DFF + fcs[0] * 128, DFF + (fcs[-1] + 1) * 128)]

        pf = hoisted.pop(("ffn", s, l), None)
        if pf is not None:
            hs = [pf[0]] + [next_wb() for _ in subs[1:]]
            wvs[0] = pf[1]
        else:
            hs = [next_wb() for _ in subs]
            wvs[0] = load_w(hs[0], w_up[l], 0, KC, crs(subs[0]))
        for si, fcs in enumerate(subs):
            if si + 1 < len(subs):
                wvs[si + 1] = load_w(hs[si + 1], w_up[l], 0, KC, crs(subs[si + 1]))
            nf = len(fcs)
            pend = {}

            def epi(oc, ps, pk, tinfo, ti, fcs=fcs, nf=nf, pend=pend):
                lo, n, olo, on, coff = tinfo
                if oc < nf:
                    fc = fcs[oc]
                    w0 = CONVW[:, (l * 3 + 0) * FC + fc:(l * 3 + 0) * FC + fc + 1]
                    w1 = CONVW[:, (l * 3 + 1) * FC + fc:(l * 3 + 1) * FC + fc + 1]
                    w2 = CONVW[:, (l * 3 + 2) * FC + fc:(l * 3 + 2) * FC + fc + 1]
                    cb = CONVB[:, l * FC + fc:l * FC + fc + 1]
                    u_, uk = sf()
                    ck = (("CONVW", l, 0), ("CONVW", l, 1), ("CONVW", l, 2), ("CONVB", l))
                    sch.op("dve", lambda e: e.tensor_scalar(u_[:, 0:on], ps[:, 0:on], w0, None, op0=ALU.mult), reads=(pk,) + ck, writes=(uk,))
                    sch.op("dve", lambda e: e.scalar_tensor_tensor(u_[:, 0:on], ps[:, 1:1 + on], w1, u_[:, 0:on], ALU.mult, ALU.add),
                           reads=(pk, uk) + ck, writes=(uk,))
                    sch.op("dve", lambda e: e.scalar_tensor_tensor(u_[:, 0:on], ps[:, 2:2 + on], w2, u_[:, 0:on], ALU.mult, ALU.add),
                           reads=(pk, uk) + ck, writes=(uk,))
                    sch.op("act", lambda e: e.activation(out=u_[:, 0:on], in_=u_[:, 0:on], func=AF.Gelu, bias=cb), reads=(uk,) + ck, writes=(uk,))
                    pend[oc] = (u_, uk)
                else:
                    fc = fcs[oc - nf]
                    u_, uk = pend.pop(oc - nf)
                    g_, gk = sh()
                    sch.op("dve", lambda e: e.tensor_tensor(g_[:, 0:on], u_[:, 0:on], ps[:, 1:1 + on], ALU.mult), reads=(uk, pk), writes=(gk,))
                    sch.dma(STQ, GT[s][fc][:, olo:olo + on], g_[:, 0:on], reads=(gk,), writes=(("GT", s),))

            order = [o for f in range(nf) for o in (f, nf + f)]
            linear_pass(s, HT[s], ("HT", s), KC, wvs[si], hs[si], crs(fcs), 2 * nf, epi, T, tiles=tiles, oc_order=order)

    def linear_pass_ab(s, wv, h, col_ranges, nf, epi, T, tiles):
        S = S_list[s]
        for ti, (lo, n, olo, on, coff) in enumerate(tiles):
            u = ti % 2
            buf = RA[u][:, 0:KC * T].rearrange("p (k t) -> p k t", k=KC)
            bkey = ("RA", u)
            vlo, vhi = max(lo, 0), min(lo + n, S)
            if vlo > lo or vhi < lo + n:
                sch.op("dve", lambda e, buf=buf, n=n: e.memset(buf[:, :, 0:n], 0.0), writes=(bkey,))
            sch.dma("sp", buf[:, :, vlo - lo:vhi - lo], HT[s][:, :, vlo:vhi].rearrange("k p t -> p k t"),
                    reads=(("HT", s),), writes=(bkey,))
            for f in range(nf):
                for oc in (f, nf + f):
                    ps, pk = psb()
                    for kc in range(KC):
                        mm(ps[:, 0:n], wv[:, kc, oc * 128:(oc + 1) * 128], buf[:, kc, 0:n], kc == 0, kc == KC - 1,
                           reads=tuple(wkeys(h, kc, col_ranges)) + (bkey,), writes=(pk,))
                    epi(oc, ps, pk, (lo, n, olo, on, coff))

    hoisted = {}

    def hoist_qkv(s, l):
        h = next_wb()
        hoisted[("qkv", s, l)] = (h, load_w(h, wqkv[l % 4], 0, KC, [(0, 1536)]))

    def hoist_ffn(s, l):
        h = next_wb()
        hoisted[("ffn", s, l)] = (h, load_w(h, w_up[l], 0, KC, [(0, 768), (DFF, DFF + 768)]))

    for s in range(NSEQ):
        phase_transpose_in(s)
    for l in range(depth):
        for s in range(NSEQ):
            hoist_qkv(s, l)
            phase_norm(s, modp(s, l, 0), modp(s, l, 1))
            phase_qkv(s, l)
            phase_attn(s, l)
            phase_resid(s, OT[s], ("OT", s), KC, wo[l % 4], modp(s, l, 2), 512, 8, "RA")
            hoist_ffn(s, l)
            phase_norm(s, modp(s, l, 3), modp(s, l, 4))
            phase_ffn_up(s, l)
            phase_resid(s, GT[s], ("GT", s), FC, w_down[l], modp(s, l, 5), 256, 4, "RB")
    for s in range(NSEQ):
        phase_norm(s, NFIN, None, final=True)

    sch.finish()
    sch.emit()
    es.close()
    return nc


_PROG_CACHE = {}


def _get_prog(S_list, depth=DEPTH):
    key = (tuple(S_list), depth)
    if key not in _PROG_CACHE:
        _PROG_CACHE[key] = build_program(list(S_list), depth)
    return _PROG_CACHE[key]


def run_cores(xs_per_core, cs_per_core, weights, S_list, depth=DEPTH):
    nc = _get_prog(S_list, depth)
    consts = _const_tables()
    f = lambda a: np.ascontiguousarray(np.asarray(a, dtype=np.float32))
    shared = {
        "norm_attn": f(weights["norm_attn"]).reshape(DEPTH, KC, 128),
        "norm_ffn": f(weights["norm_ffn"]).reshape(DEPTH, KC, 128),
        "w_ada": f(weights["w_ada"]),
        "b_ada": f(weights["b_ada"]).reshape(DEPTH, 96, 128),
        "a_wqkv": f(weights["a_wqkv"])[0], "b_wqkv": f(weights["b_wqkv"])[0],
        "c_wqkv": f(weights["c_wqkv"])[0], "d_wqkv": f(weights["d_wqkv"])[0],
        "a_wo": f(weights["a_wo"])[0], "b_wo": f(weights["b_wo"])[0],
        "c_wo": f(weights["c_wo"])[0], "d_wo": f(weights["d_wo"])[0],
        "a_q_norm": f(weights["a_q_norm"]).reshape(1, 128),
        "a_k_norm": f(weights["a_k_norm"]).reshape(1, 128),
        "b_sink": f(weights["b_sink"]).reshape(1, 16),
        "c_lambda": f(weights["c_lambda"]).reshape(4, 128),
        "c_subln": f(weights["c_subln"]).reshape(2, 128),
        "ffn_w_up": f(weights["ffn_w_up"]),
        "ffn_conv_w": f(weights["ffn_conv_w"]).reshape(DEPTH, 3, FC, 128),
        "ffn_conv_b": f(weights["ffn_conv_b"]).reshape(DEPTH, FC, 128),
        "ffn_w_down": f(weights["ffn_w_down"]),
        "norm_final": f(weights["norm_final"]).reshape(KC, 128),
    }
    shared.update(consts)
    in_maps = []
    for c in range(len(xs_per_core)):
        m = dict(shared)
        for s in range(len(S_list)):
            m["x%d" % s] = f(xs_per_core[c][s])
            m["c%d" % s] = f(cs_per_core[c][s]).reshape(KC, 128)
        in_maps.append(m)
    res = run_bass_kernel_spmd(nc, in_maps, core_ids=list(range(len(in_maps))))
    return [[np.asarray(r["y%d" % s]) for s in range(len(S_list))] for r in res.results]


def kernel(x_prompt, x_sample, c_prompt, c_sample, **weights):
    x_prompt = np.asarray(x_prompt, dtype=np.float32)
    x_sample = np.asarray(x_sample, dtype=np.float32)
    c_prompt = np.asarray(c_prompt, dtype=np.float32)
    c_sample = np.asarray(c_sample, dtype=np.float32)
    nb_p = x_prompt.shape[0]
    xs, cs = [], []
    for c in range(N_CORES):
        xs.append([x_sample[c], x_prompt[c % nb_p]])
        cs.append([c_sample[c], c_prompt[c % nb_p]])
    outs = run_cores(xs, cs, weights, [x_sample.shape[1], x_prompt.shape[1]])
    y_sample = np.stack([outs[c][0] for c in range(N_CORES)], axis=0)
    y_prompt = np.stack([outs[c][1] for c in range(nb_p)], axis=0)
    return (y_prompt.astype(np.float32), y_sample.astype(np.float32))
```
